# Optimizing a Trainium2 kernel written in Bass

```python
import numpy as np
import jax, jax.numpy as jnp
from jax import lax

D_MODEL = 1024
BATCH = 4
SEQ = 8192
DEPTH = 1

D_CONV = D_MODEL
CONV_K = 3
HEAD_DIM = 64
N_HEADS = D_MODEL // HEAD_DIM
N_KV_HEADS = 4
GROUP = N_HEADS // N_KV_HEADS
D_ATTN = N_HEADS * HEAD_DIM
D_KV = N_KV_HEADS * HEAD_DIM
WINDOW = 128
BLOCK = 128
ROPE_THETA = 10000.0
D_FF = 2816
N_MOD = 9
EPS = 1e-6
ADA_INIT = 0.5
NEG_INF = -1e30

IN_SPLITS = [D_CONV, D_CONV, D_CONV, D_ATTN, D_KV, D_KV, D_MODEL, D_MODEL]
IN_OFFSETS = [int(o) for o in np.cumsum(IN_SPLITS)[:-1]]
D_IN = int(sum(IN_SPLITS))

kernel_name = "macaron_conv_swa_sink_hybrid"


def rms_norm(x, g):
    xf = x.astype(jnp.float32)
    y = xf * lax.rsqrt(jnp.mean(xf * xf, axis=-1, keepdims=True) + EPS)
    return (y * g.astype(jnp.float32)).astype(x.dtype)


def modulate(h, shift, scale):
    return h * (1 + scale[:, None, :]) + shift[:, None, :]


def swiglu(h, w_gu, w_down):
    a, b = jnp.split(h @ w_gu, 2, axis=-1)
    return (jax.nn.silu(a) * b) @ w_down


def rope_tables(seq):
    inv = 1.0 / (ROPE_THETA ** (jnp.arange(0, HEAD_DIM, 2, dtype=jnp.float32) / HEAD_DIM))
    ang = jnp.arange(seq, dtype=jnp.float32)[:, None] * inv[None, :]
    return jnp.cos(ang), jnp.sin(ang)


def apply_rope(t, cos, sin):
    t1, t2 = jnp.split(t.astype(jnp.float32), 2, axis=-1)
    c = cos[None, :, None, :]
    s = sin[None, :, None, :]
    return jnp.concatenate([t1 * c - t2 * s, t2 * c + t1 * s], axis=-1).astype(t.dtype)


def short_conv(u, w):
    s = u.shape[1]
    up = jnp.pad(u, ((0, 0), (CONV_K - 1, 0), (0, 0)))
    out = up[:, 0:s] * w[0]
    for j in range(1, CONV_K):
        out = out + up[:, j:j + s] * w[j]
    return out


def sliding_window_attention(q, k, v, sinks):
    bsz, s, _, _ = q.shape
    nb = s // BLOCK
    qb = q.reshape(bsz, nb, BLOCK, N_KV_HEADS, GROUP, HEAD_DIM)

    def band(t):
        tp = jnp.pad(t, ((0, 0), (BLOCK, 0), (0, 0), (0, 0)))
        tp = tp.reshape(bsz, nb + 1, BLOCK, N_KV_HEADS, HEAD_DIM)
        return jnp.concatenate([tp[:, :-1], tp[:, 1:]], axis=2)

    kb, vb = band(k), band(v)
    scores = jnp.einsum('bnqhgd,bnkhd->bnhgqk', qb, kb,
                        preferred_element_type=jnp.float32) * (HEAD_DIM ** -0.5)
    qi = jnp.arange(BLOCK)[:, None]
    kj = jnp.arange(2 * BLOCK)[None, :]
    diff = BLOCK + qi - kj
    in_window = (diff >= 0) & (diff < WINDOW)
    k_pos = (jnp.arange(nb)[:, None] - 1) * BLOCK + jnp.arange(2 * BLOCK)[None, :]
    valid = in_window[None] & (k_pos >= 0)[:, None, :]
    scores = jnp.where(valid[None, :, None, None], scores, NEG_INF)
    sink = sinks.astype(jnp.float32).reshape(N_KV_HEADS, GROUP)[None, None, :, :, None, None]
    m = jnp.maximum(jnp.max(scores, axis=-1, keepdims=True), sink)
    p = jnp.exp(scores - m)
    denom = jnp.sum(p, axis=-1, keepdims=True) + jnp.exp(sink - m)
    probs = (p / denom).astype(v.dtype)
    out = jnp.einsum('bnhgqk,bnkhd->bnqhgd', probs, vb)
    return out.reshape(bsz, s, D_ATTN)


def setup_inputs(seed: int = 0) -> dict:
    key = jax.random.key(seed)
    ks = jax.random.split(key, 24)
    f32 = jnp.float32

    def nrm(k, shape, fan_in, mult=1.0):
        return jax.random.normal(k, shape, f32) * (mult * fan_in ** -0.5)

    def gain(k, shape):
        return 1.0 + 0.05 * jax.random.normal(k, shape, f32)

    L = DEPTH
    return {
        "x": jax.random.normal(ks[0], (BATCH, SEQ, D_MODEL), f32),
        "c": jax.random.normal(ks[1], (BATCH, D_MODEL), f32),
        "w_ada": nrm(ks[2], (L, D_MODEL, N_MOD * D_MODEL), D_MODEL, ADA_INIT),
        "b_ada": 0.02 * jax.random.normal(ks[3], (L, N_MOD * D_MODEL), f32),
        "g_ffn1": gain(ks[4], (L, D_MODEL)),
        "w1_gu": nrm(ks[5], (L, D_MODEL, 2 * D_FF), D_MODEL),
        "w1_down": nrm(ks[6], (L, D_FF, D_MODEL), D_FF),
        "g_mix": gain(ks[7], (L, D_MODEL)),
        "w_in": nrm(ks[8], (L, D_MODEL, D_IN), D_MODEL),
        "conv_w": nrm(ks[9], (L, CONV_K, D_CONV), CONV_K),
        "w_conv_proj": nrm(ks[10], (L, D_CONV, D_MODEL), D_CONV),
        "w_attn_proj": nrm(ks[11], (L, D_ATTN, D_MODEL), D_ATTN),
        "sinks": jax.random.normal(ks[12], (L, N_HEADS), f32),
        "w_out": nrm(ks[13], (L, D_MODEL, D_MODEL), D_MODEL),
        "g_ffn2": gain(ks[14], (L, D_MODEL)),
        "w2_gu": nrm(ks[15], (L, D_MODEL, 2 * D_FF), D_MODEL),
        "w2_down": nrm(ks[16], (L, D_FF, D_MODEL), D_FF),
        "g_final": gain(ks[17], (D_MODEL,)),
    }


def reference(x, c, w_ada, b_ada, g_ffn1, w1_gu, w1_down, g_mix, w_in, conv_w,
              w_conv_proj, w_attn_proj, sinks, w_out, g_ffn2, w2_gu, w2_down, g_final):
    bsz, s, _ = x.shape
    cos, sin = rope_tables(s)
    c_act = jax.nn.silu(c)
    for l in range(DEPTH):
        mods = jnp.split(c_act @ w_ada[l] + b_ada[l], N_MOD, axis=-1)
        sh1, sc1, gt1, sh2, sc2, gt2, sh3, sc3, gt3 = mods

        h = modulate(rms_norm(x, g_ffn1[l]), sh1, sc1)
        x = x + 0.5 * gt1[:, None, :] * swiglu(h, w1_gu[l], w1_down[l])

        h = modulate(rms_norm(x, g_mix[l]), sh2, sc2)
        proj = h @ w_in[l]
        b_g, c_g, u, q, k, v, z_conv, z_attn = jnp.split(proj, IN_OFFSETS, axis=-1)

        y_conv = (b_g * short_conv(c_g * u, conv_w[l])) @ w_conv_proj[l]

        q = apply_rope(q.reshape(bsz, s, N_HEADS, HEAD_DIM), cos, sin)
        k = apply_rope(k.reshape(bsz, s, N_KV_HEADS, HEAD_DIM), cos, sin)
        v = v.reshape(bsz, s, N_KV_HEADS, HEAD_DIM)
        y_attn = sliding_window_attention(q, k, v, sinks[l]) @ w_attn_proj[l]

        merged = jax.nn.sigmoid(z_conv) * y_conv + jax.nn.sigmoid(z_attn) * y_attn
        x = x + gt2[:, None, :] * (merged @ w_out[l])

        h = modulate(rms_norm(x, g_ffn2[l]), sh3, sc3)
        x = x + 0.5 * gt3[:, None, :] * swiglu(h, w2_gu[l], w2_down[l])

    return rms_norm(x, g_final)
```

```python
from contextlib import ExitStack

import numpy as np
import concourse.bass as bass
import concourse.mybir as mybir
from concourse.bass_utils import run_bass_kernel_spmd

F32 = mybir.dt.float32
BF16 = mybir.dt.bfloat16
ALU = mybir.AluOpType
AF = mybir.ActivationFunctionType
AX = mybir.AxisListType

D = 1024
DFF = 2816
NJ = 22
NJH = 11
HD = 64
NH = 16
NKV = 4
EPS = 1e-6
NEG = -1e30
ROPE_THETA = 10000.0
RING_ELEMS = 3072
N_ADA_LOADS = 24

O_B, O_C, O_U, O_Q, O_K, O_V, O_ZC, O_ZA = 0, 1024, 2048, 3072, 4096, 4352, 4608, 5632


class Cfg:
    def __init__(self, ntc=32, groups=(8, 8, 8, 8), nring=4, ntmp=8, stop=99):
        assert sum(groups) == ntc
        self.stop = stop
        self.ntc = ntc
        self.groups = list(groups)
        self.nsmax = max(groups) + 1
        self.ncmax = self.nsmax * 128
        self.nring = nring
        self.ntmp = ntmp


class _Op:
    __slots__ = ("eng", "fn", "deps", "dma_sem", "sigval", "needed", "group")

    def __init__(self, eng, fn, deps, dma_sem, group=None):
        self.group = group
        self.eng = eng
        self.fn = fn
        self.deps = deps
        self.dma_sem = dma_sem
        self.sigval = 0
        self.needed = False


class Tracker:
    ENGS = ("pe", "act", "dve", "pool", "sp")

    def __init__(self):
        self.ops = []
        self.last_w = {}
        self.rd_eng = {}
        self.rd_dma = {}

    def emit(self, eng, fn, reads=(), writes=(), dma_sem=None, group=None):
        i = len(self.ops)
        deps = set()
        for r in reads:
            w = self.last_w.get(r)
            if w is not None:
                deps.add(w)
        for wr in writes:
            w = self.last_w.get(wr)
            if w is not None:
                deps.add(w)
            for d in self.rd_eng.get(wr, {}).values():
                deps.add(d)
            for d in self.rd_dma.get(wr, ()):
                deps.add(d)
        if eng == "pe":
            deps = {d for d in deps if not (self.ops[d].eng == "pe" and self.ops[d].dma_sem is None)}
        if group is not None:
            deps = {d for d in deps if not (self.ops[d].group == group and self.ops[d].dma_sem == dma_sem)}
        self.ops.append(_Op(eng, fn, deps, dma_sem, group))
        for r in reads:
            if dma_sem is not None:
                self.rd_dma.setdefault(r, []).append(i)
            else:
                self.rd_eng.setdefault(r, {})[eng] = i
        for wr in writes:
            self.last_w[wr] = i
            self.rd_eng[wr] = {}
            self.rd_dma[wr] = []
        return i

    def finalize(self):
        ops = self.ops
        for op in ops:
            for d in op.deps:
                ops[d].needed = True
        cnt = {e: 0 for e in self.ENGS}
        dcnt = {}
        for op in ops:
            if op.dma_sem is not None:
                dcnt[op.dma_sem] = dcnt.get(op.dma_sem, 0) + 16
                op.sigval = dcnt[op.dma_sem]
            elif op.needed:
                cnt[op.eng] += 1
                op.sigval = cnt[op.eng]
        gmax = {}
        for op in ops:
            if op.group is not None:
                gk = (op.dma_sem, op.group)
                gmax[gk] = max(gmax.get(gk, 0), op.sigval)
        for op in ops:
            if op.group is not None:
                op.sigval = gmax[(op.dma_sem, op.group)]
        streams = {e: [] for e in self.ENGS}
        known = {e: {} for e in self.ENGS}
        nwait = 0
        for op in ops:
            waits = {}
            for d in op.deps:
                dop = ops[d]
                key = ("D", dop.dma_sem) if dop.dma_sem is not None else ("E", dop.eng)
                if waits.get(key, 0) < dop.sigval:
                    waits[key] = dop.sigval
            kn = known[op.eng]
            for key, val in waits.items():
                if kn.get(key, 0) < val:
                    streams[op.eng].append(("wait", key, val))
                    kn[key] = val
                    nwait += 1
            if op.fn is not None:
                streams[op.eng].append(("op", op))
        self.dma_sems = sorted(dcnt.keys())
        self.nwait = nwait
        return streams


class Builder:
    def __init__(self, cfg):
        self.cfg = cfg
        self.nc = bass.Bass("TRN2", target_bir_lowering=False)
        self.tr = Tracker()
        self.ring_i = 0
        self.ring_loads = 0
        self.tmp_i = 0
        self.ps_rr = {}
        self.gid = 0
        self.prefetched = []

    def declare_dram(self):
        nc, cfg = self.nc, self.cfg
        di = lambda n, s: nc.dram_tensor(n, list(s), F32, kind="ExternalInput").ap()
        self.d_x = di("xin", [cfg.ntc + 1, 128, D])
        self.d_ccol = di("ccol", [128, 8])
        self.d_gcol = di("gcol", [128, 24])
        self.d_bada = di("badacol", [128, 72])
        self.d_convw = di("convw", [128, 24])
        self.d_gfin = di("gfin", [D])
        self.d_sinks = di("sinks", [NH])
        self.d_ident = di("ident", [128, 128])
        self.d_masks = di("masks", [128, 512])
        self.d_flag = di("flag", [128, 1])
        self.d_cos = di("cost", [128, (cfg.ntc + 1) * 128])
        self.d_sin = di("sint", [128, (cfg.ntc + 1) * 128])
        self.d_wada = di("wada", [N_ADA_LOADS, 128, 8 * 384])
        self.d_wgu = [di("wgu%d" % f, [NJ, 128, 8 * 256]) for f in range(2)]
        self.d_wdh = [di("wdh%d" % f, [2, 128, NJH * 1024]) for f in range(2)]
        self.d_wcv = di("wcv", [8, 128, 8 * 384])
        self.d_wq = di("wq", [8, 128, 8 * 256])
        self.d_wk = di("wk", [4, 128, 8 * 256])
        self.d_wv = di("wv", [1, 128, 8 * 256])
        self.d_wcp = di("wcp", [8, 128, 8 * 256])
        self.d_wap = di("wap", [8, 128, 8 * 256])
        self.d_wout = di("wout", [128, 8 * 1024])
        self.d_out = nc.dram_tensor("out", [cfg.ntc, 128, D], F32, kind="ExternalOutput").ap()

    def alloc(self, es):
        nc, cfg = self.nc, self.cfg
        NS, NC = cfg.nsmax, cfg.ncmax
        sb = lambda n, s, dt: es.enter_context(nc.sbuf_tensor(n, list(s), dt))
        self.xres = sb("xres", [128, NS, D], F32)
        self.hT = sb("hT", [128, 8, NC], BF16)
        self.R = sb("R", [128, 16, NC], BF16)
        self.wdh = sb("wdhb", [128, NJH, 1024], BF16)
        self.kT = sb("kT", [128, 4, NC], BF16)
        self.vv = sb("vv", [128, NS, 256], BF16)
        self.ring = sb("ring", [128, cfg.nring, RING_ELEMS], BF16)
        self.bc = sb("bc", [128, 4, D], F32)
        self.cosT = sb("cosT", [128, NC], F32)
        self.sinT = sb("sinT", [128, NC], F32)
        self.tmp = sb("tmp", [128, cfg.ntmp, 512], F32)
        self.xn = sb("xn", [128, 4, D], BF16)
        self.nt = sb("nt", [128, 2, D], F32)
        self.cu = sb("cu", [128, 1, NC + 2], F32)
        self.cucarry = sb("cucarry", [128, 8, 2], F32)
        self.Sm = sb("Sm", [128, 2, 4, 256], F32)
        self.P = sb("P", [128, 2, 4, 256], BF16)
        self.pT = sb("pT", [128, 2, 1024], BF16)
        self.ybf = sb("ybf", [128, D], BF16)
        self.ast = sb("ast", [128, 2, 4, 16], F32)
        self.stat = sb("stat", [128, 2, 2 * NS], F32)
        self.epsc = sb("epsc", [128, 1], F32)
        self.idf = sb("idf", [128, 128], F32)
        self.idb = sb("idb", [128, 128], BF16)
        self.ones = sb("ones", [128, 128], F32)
        self.ccol = sb("ccols", [128, 8], F32)
        self.cbf = sb("cbf", [128, 8], BF16)
        self.gcol = sb("gcols", [128, 24], F32)
        self.bada = sb("badas", [128, 72], F32)
        self.modcol = sb("modcol", [128, 72], F32)
        self.gs = sb("gs", [128, 6, 8], F32)
        self.convw = sb("convws", [128, 24], F32)
        self.sinkbc = sb("sinkbc", [128, 2, NH], F32)
        self.masks = sb("maskss", [128, 2, 256], F32)
        self.flag = sb("flags", [128, 1], F32)
        self.ps = es.enter_context(nc.psum_tensor("ps", [128, 4096], F32))

    def bank(self, b, n=512):
        return self.ps[:, b * 512:b * 512 + n]

    def bank_bf(self, b):
        return self.ps[:, b * 512:(b + 1) * 512].bitcast(BF16)

    def rr(self, name, options):
        i = self.ps_rr.get(name, 0)
        self.ps_rr[name] = i + 1
        return options[i % len(options)]

    def tmp512(self):
        i = self.tmp_i % self.cfg.ntmp
        self.tmp_i += 1
        return self.tmp[:, i, :], ("tmp", i)

    def E(self, eng, fn, reads=(), writes=(), dma_sem=None, group=None):
        return self.tr.emit(eng, fn, reads, writes, dma_sem, group)

    def newgroup(self, bump=False):
        if bump:
            self.gid += 1
        return self.gid

    def _ring_issue(self, src, nelem):
        i = self.ring_i % self.cfg.nring
        self.ring_i += 1
        dst = self.ring[:, i, 0:nelem]
        self.E("pool", lambda e, dst=dst, src=src: e.dma_start(out=dst, in_=src),
               reads=(), writes=[("ring", i)], dma_sem="ring%d" % i)
        return self.ring[:, i, :], ("ring", i)

    def ring_load(self, src, nelem, tag=None):
        if self.prefetched:
            ptag, slot, key = self.prefetched.pop(0)
            assert ptag == tag and tag is not None, (ptag, tag)
            return slot, key
        return self._ring_issue(src, nelem)

    def prefetch(self, loads):
        assert not self.prefetched
        for src, nelem, tag in loads[:self.cfg.nring]:
            slot, key = self._ring_issue(src, nelem)
            self.prefetched.append((tag, slot, key))

    def nchunks(self, lo, ns):
        out = []
        c = lo * 128
        end = ns * 128
        while c < end:
            n = min(512, end - c)
            out.append((c, c + n))
            c += n
        return out

    @staticmethod
    def slots_of(c0, c1):
        return list(range(c0 // 128, (c1 + 127) // 128))

    def setup(self):
        E = self.E
        ld = lambda dst, src, w: E("sp", lambda e: e.dma_start(out=dst, in_=src), writes=[w], dma_sem="const", group=0)
        ld(self.ccol[:], self.d_ccol, "ccol")
        ld(self.bada[:], self.d_bada, "bada")
        ld(self.gcol[:], self.d_gcol, "gcol")
        ld(self.idf[:], self.d_ident, "idf")
        ld(self.convw[:], self.d_convw, "convw")
        ld(self.masks[:].rearrange("p a b -> p (a b)"), self.d_masks, "masks")
        ld(self.flag[:], self.d_flag, "flag")
        ld(self.sinkbc[:, 0, :], self.d_sinks.partition_broadcast(128), "sink0")
        ld(self.bc[:, 3, :], self.d_gfin.partition_broadcast(128), ("bc", 3))
        E("dve", lambda e: e.tensor_copy(out=self.idb[:], in_=self.idf[:]), reads=["idf"], writes=["idb"])
        E("dve", lambda e: e.memset(self.ones[:], 1.0), writes=["ones"])
        E("dve", lambda e: e.memset(self.epsc[:], EPS), writes=["epsc"])
        E("dve", lambda e: e.memset(self.cucarry[:], 0.0), writes=["cucarry"])
        E("dve", lambda e: e.tensor_scalar(out=self.sinkbc[:, 1, :], in0=self.sinkbc[:, 0, :], scalar1=-1.0,
                                           scalar2=None, op0=ALU.mult), reads=["sink0"], writes=["sink1"])
        E("act", lambda e: e.activation(out=self.ccol[:], in_=self.ccol[:], func=AF.Silu),
          reads=["ccol"], writes=["ccol"])
        E("dve", lambda e: e.tensor_copy(out=self.cbf[:], in_=self.ccol[:]), reads=["ccol"], writes=["cbf"])

    def ada_load(self, l):
        E = self.E
        slot, key = self.ring_load(self.d_wada[l], 8 * 384)
        sv = slot[:, 0:8 * 384].rearrange("p (k c) -> p k c", k=8)
        for mi in range(3):
            mc = 3 * l + mi
            for k in range(8):
                E("pe", lambda e, mc=mc, k=k, mi=mi, sv=sv: e.matmul(
                    out=self.ps[:, 7 * 512 + mc:7 * 512 + mc + 1], lhsT=sv[:, k, mi * 128:(mi + 1) * 128],
                    rhs=self.cbf[:, k:k + 1], start=(k == 0), stop=(k == 7)),
                  reads=[key, "cbf"], writes=[("ps", 7)])
        E("dve", lambda e: e.tensor_tensor(out=self.modcol[:, 3 * l:3 * l + 3], in0=self.ps[:, 7 * 512 + 3 * l:7 * 512 + 3 * l + 3],
                                           in1=self.bada[:, 3 * l:3 * l + 3], op=ALU.add),
          reads=[("ps", 7), "bada"], writes=[("modcol", l)])

    def ada_finish(self, n):
        E = self.E
        c0 = 24 * n
        mrd = [("modcol", l) for l in range(8 * n, 8 * n + 8)]
        E("dve", lambda e: e.tensor_copy(out=self.gs[:, 2 * n + 1, :], in_=self.modcol[:, c0:c0 + 8]),
          reads=mrd, writes=[("gs", 2 * n + 1)])
        E("dve", lambda e: e.scalar_tensor_tensor(out=self.gs[:, 2 * n, :], in0=self.modcol[:, c0 + 8:c0 + 16], scalar=1.0,
                                                  in1=self.gcol[:, 8 * n:8 * n + 8], op0=ALU.add, op1=ALU.mult),
          reads=mrd + ["gcol"], writes=[("gs", 2 * n)])
        gsc = 1.0 if n == 1 else 0.5
        for k in range(8):
            E("dve", lambda e, k=k: e.tensor_scalar(out=self.nt[:, 0, k * 128:(k + 1) * 128], in0=self.idf[:],
                                                    scalar1=self.modcol[:, c0 + 16 + k:c0 + 17 + k], scalar2=gsc,
                                                    op0=ALU.mult, op1=ALU.mult),
              reads=mrd + ["idf"], writes=[("nt", 0)])
        for half in range(2):
            b = 5 + half
            E("pe", lambda e, half=half, b=b: e.matmul(
                out=self.bank(b), lhsT=self.ones[:],
                rhs=self.nt[:, 0, half * 512:(half + 1) * 512], start=True, stop=True),
              reads=["ones", ("nt", 0)], writes=[("ps", b)])
            E("act", lambda e, half=half, b=b: e.activation(out=self.bc[:, n, half * 512:(half + 1) * 512],
                                                            in_=self.bank(b), func=AF.Copy),
              reads=[("ps", b)], writes=[("bc", n)])

    def zero_ss(self):
        nsc = 2 * self.cfg.nsmax
        self.E("dve", lambda e: e.memset(self.stat[:, 0, :], 0.0), writes=[("ss", c) for c in range(nsc)])

    def rstd_tile(self, t, junk, jkey, col=None):
        E = self.E
        col = t if col is None else col
        E("act", lambda e: e.activation(out=junk, in_=self.xres[:, t, :], func=AF.Square,
                                        accum_out=self.stat[:, 0, col:col + 1]),
          reads=[("x", t), ("ss", col)], writes=[("ss", col), jkey])
        E("act", lambda e: e.activation(out=self.stat[:, 1, col:col + 1], in_=self.stat[:, 0, col:col + 1], func=AF.Sqrt,
                                        bias=self.epsc[:, 0:1], scale=1.0 / D),
          reads=[("ss", col), "epsc"], writes=[("rstd", col)])
        E("dve", lambda e: e.reciprocal(out=self.stat[:, 1, col:col + 1], in_=self.stat[:, 1, col:col + 1]),
          reads=[("rstd", col)], writes=[("rstd", col)])

    def norm_a(self, t):
        E = self.E
        xi = self.rr("xn", [0, 1, 2, 3])
        self.rstd_tile(t, self.xn[:, xi, :], ("xn", xi))
        E("act", lambda e: e.activation(out=self.xn[:, xi, :], in_=self.xres[:, t, :], func=AF.Copy,
                                        scale=self.stat[:, 1, t:t + 1]),
          reads=[("x", t), ("rstd", t)], writes=[("xn", xi)])
        return xi

    def norm_b(self, t, n, xi, banks=(0, 1, 2, 3)):
        E = self.E
        bb = self.rr("psT", list(banks))
        pv = self.bank_bf(bb)
        for k in range(8):
            E("pe", lambda e, k=k: e.transpose(out=pv[:, k * 128:(k + 1) * 128], in_=self.xn[:, xi, k * 128:(k + 1) * 128],
                                               identity=self.idb[:]),
              reads=[("xn", xi), "idb"], writes=[("ps", bb)])
        ni = self.rr("nt", [0, 1])
        ntv = self.nt[:, ni, :].rearrange("p (k c) -> p k c", k=8)
        E("dve", lambda e: e.tensor_tensor(out=ntv, in0=pv.rearrange("p (k c) -> p k c", k=8),
                                           in1=self.gs[:, 2 * n, :].unsqueeze(2).to_broadcast([128, 8, 128]), op=ALU.mult),
          reads=[("ps", bb), ("gs", 2 * n)], writes=[("nt", ni)])
        E("dve", lambda e: e.tensor_tensor(out=self.hT[:, :, t * 128:(t + 1) * 128], in0=ntv,
                                            in1=self.gs[:, 2 * n + 1, :].unsqueeze(2).to_broadcast([128, 8, 128]), op=ALU.add),
          reads=[("nt", ni), ("gs", 2 * n + 1)], writes=[("hT", t)])

    def norm_tile(self, t, n, banks=(0, 1, 2, 3)):
        xi = self.norm_a(t)
        self.norm_b(t, n, xi, banks)

    def norm_pipe(self, n, delay=1, lazy_from=10 ** 9):
        pend = []

        def after_tile(t):
            pend.append((t, self.norm_a(t)))
            while len(pend) > delay and pend[0][0] < lazy_from:
                t0, xi0 = pend.pop(0)
                self.norm_b(t0, n, xi0)

        def ensure(tmax, banks=(4, 5, 6, 7)):
            while pend and pend[0][0] <= tmax:
                t0, xi0 = pend.pop(0)
                self.norm_b(t0, n, xi0, banks)

        def flush():
            ensure(10 ** 9, (0, 1, 2, 3))

        return after_tile, ensure, flush

    def ffn(self, g, fi, lo, extra=None, after_tile=None, next_loads=None, ensure=None):
        E = self.E
        ns = self.cfg.groups[g] + 1
        slots = list(range(lo, ns))
        n = 0 if fi == 0 else 2
        chunks = self.nchunks(lo, ns)
        actT = self.R
        for hh in range(2):
            loaded = {}

            def a5_load(jj, hh=hh):
                j = hh * NJH + jj
                slot, key = self.ring_load(self.d_wgu[fi][j], 8 * 256, ("wgu", fi, j))
                loaded[jj] = (slot[:, 0:2048].rearrange("p (k c) -> p k c", k=8), key)
                if jj == 2:
                    self.gid += 1
                    for part, (a, b_) in enumerate([(0, 4), (4, 8), (8, 11)]):
                        E("pool", lambda e, a=a, b_=b_, hh=hh: e.dma_start(
                            out=self.wdh[:, a:b_, :].rearrange("p a b -> p (a b)"),
                            in_=self.d_wdh[fi][hh][:, a * 1024:b_ * 1024]),
                          writes=[("wdh", part)], dma_sem="wdh", group=self.newgroup())

            def a5_step(jj, c0, c1, hh=hh):
                sv, key = loaded[jj]
                nn = c1 - c0
                if ensure is not None and hh == 0:
                    ensure(self.slots_of(c0, c1)[-1])
                ba, bb = self.rr("psA5", [(0, 1), (2, 3)])
                rd = [key] + [("hT", t) for t in self.slots_of(c0, c1)]
                for half, b in ((0, ba), (1, bb)):
                    for k in range(8):
                        E("pe", lambda e, half=half, b=b, k=k, sv=sv, c0=c0, c1=c1, nn=nn: e.matmul(
                            out=self.bank(b, nn), lhsT=sv[:, k, half * 128:(half + 1) * 128],
                            rhs=self.hT[:, k, c0:c1], start=(k == 0), stop=(k == 7)),
                          reads=rd, writes=[("ps", b)])
                tp, tk = self.tmp512()
                E("act", lambda e, tp=tp, ba=ba, nn=nn: e.activation(out=tp[:, 0:nn], in_=self.bank(ba, nn), func=AF.Silu),
                  reads=[("ps", ba)], writes=[tk])
                E("dve", lambda e, tp=tp, bb=bb, nn=nn, jj=jj, c0=c0, c1=c1: e.tensor_tensor(
                    out=actT[:, jj, c0:c1], in0=tp[:, 0:nn], in1=self.bank(bb, nn), op=ALU.mult),
                  reads=[tk, ("ps", bb)], writes=[("R", jj, t) for t in self.slots_of(c0, c1)])

            nskew = self.cfg.nring if (hh == 0 and ensure is not None and len(chunks) > 1) else 0
            if nskew:
                for jj in range(nskew):
                    a5_load(jj)
                for ci, (c0, c1) in enumerate(chunks):
                    for jj in range(nskew):
                        a5_step(jj, c0, c1)
            for jj in range(nskew, NJH):
                a5_load(jj)
                for (c0, c1) in chunks:
                    a5_step(jj, c0, c1)
            wkeys = [("wdh", p) for p in range(3)]
            if hh == 1 and after_tile is not None:
                self.zero_ss()
            if extra is None:
                if hh == 0:
                    self.prefetch([(self.d_wgu[fi][j_], 8 * 256, ("wgu", fi, j_)) for j_ in range(NJH, NJH + 3)])
                elif next_loads:
                    self.prefetch(next_loads)
            for t in slots:
                b0, b1 = self.rr("psA6", [(4, 5), (6, 7)])
                for jj in range(NJH):
                    for oh, b in ((0, b0), (1, b1)):
                        E("pe", lambda e, jj=jj, oh=oh, b=b, t=t: e.matmul(
                            out=self.bank(b), lhsT=actT[:, jj, t * 128:(t + 1) * 128],
                            rhs=self.wdh[:, jj, oh * 512:(oh + 1) * 512], start=(jj == 0), stop=(jj == NJH - 1)),
                          reads=wkeys + [("R", jj, t)], writes=[("ps", b)])
                for oh, b in ((0, b0), (1, b1)):
                    tp, tk = self.tmp512()
                    E("dve", lambda e, tp=tp, b=b, oh=oh: e.tensor_tensor(
                        out=tp, in0=self.bank(b), in1=self.bc[:, n, oh * 512:(oh + 1) * 512], op=ALU.mult),
                      reads=[("ps", b), ("bc", n)], writes=[tk])
                    E("dve", lambda e, tp=tp, oh=oh, t=t: e.tensor_tensor(
                        out=self.xres[:, t, oh * 512:(oh + 1) * 512], in0=self.xres[:, t, oh * 512:(oh + 1) * 512],
                        in1=tp, op=ALU.add),
                      reads=[tk, ("x", t)], writes=[("x", t)])
                if extra is not None:
                    extra()
                if hh == 1 and after_tile is not None:
                    after_tile(t)
            if hh == 0 and extra is not None:
                extra(upto=15)

    def rope_evac(self, bq, bs, dst, c0, c1):
        E = self.E
        nn = c1 - c0
        t1, k1 = self.tmp512()
        t2, k2 = self.tmp512()
        E("dve", lambda e: e.tensor_tensor(out=t1[:, 0:nn], in0=self.bank(bq, nn), in1=self.cosT[:, c0:c1], op=ALU.mult),
          reads=[("ps", bq), "tab"], writes=[k1])
        E("dve", lambda e: e.tensor_tensor(out=t2[:, 0:nn], in0=self.bank(bs, nn), in1=self.sinT[:, c0:c1], op=ALU.mult),
          reads=[("ps", bs), "tab"], writes=[k2])
        return t1, k1, t2, k2

    def mixer(self, g, after_tile=None, ensure=None):
        E = self.E
        cfg = self.cfg
        nt = cfg.groups[g]
        ns = nt + 1
        lo = 0 if g == 0 else 1
        tile0 = sum(cfg.groups[:g])
        Y = self.R
        MB = 8
        for dst, src in ((self.cosT, self.d_cos), (self.sinT, self.d_sin)):
            E("sp", lambda e, dst=dst, src=src: e.dma_start(out=dst[:, 0:ns * 128], in_=src[:, tile0 * 128:(tile0 + ns) * 128]),
              writes=["tab"], dma_sem="tab", group=("tab", g))
        if g > 0:
            pns = cfg.groups[g - 1] + 1
            E("dve", lambda e: e.tensor_copy(out=self.kT[:, :, 0:128], in_=self.kT[:, :, (pns - 1) * 128:pns * 128]),
              reads=[("kT", gg, pns - 1) for gg in range(4)], writes=[("kT", gg, 0) for gg in range(4)])
            E("dve", lambda e: e.tensor_copy(out=self.vv[:, 0, :], in_=self.vv[:, pns - 1, :]),
              reads=[("v", pns - 1)], writes=[("v", 0)])
        chunks_lo = self.nchunks(lo, ns)
        chunks_1 = self.nchunks(1, ns)
        for j in range(8):
            slot, key = self.ring_load(self.d_wcv[j], 8 * 384, ("wcv", j))
            if j == 2:
                E("pool", lambda e: e.dma_start(out=self.wdh[:, 0:8, :].rearrange("p a b -> p (a b)"), in_=self.d_wout),
                  writes=[("wdh", p) for p in range(3)], dma_sem="wdh", group=self.newgroup(True))
            sv = slot[:, 0:3072].rearrange("p (k c) -> p k c", k=8)
            ci = 0
            cu = self.cu[:, ci, :]
            if g > 0:
                E("dve", lambda e, cu=cu, j=j: e.tensor_copy(out=cu[:, 128:130], in_=self.cucarry[:, j, :]),
                  reads=[("cucarry", j)], writes=[("cu", ci)])
            else:
                E("dve", lambda e, cu=cu: e.memset(cu[:, 0:2], 0.0), writes=[("cu", ci)])
            for ich, (c0, c1) in enumerate(chunks_lo):
                nn = c1 - c0
                if ensure is not None and j == 0:
                    ensure(self.slots_of(c0, c1)[-1], (6, 7))
                b_b, b_c, b_u = self.rr("psCV", [(0, 1, 2), (3, 4, 5)])
                rd = [key] + [("hT", t) for t in self.slots_of(c0, c1)]
                for m, b in ((0, b_b), (1, b_c), (2, b_u)):
                    for k in range(8):
                        E("pe", lambda e, m=m, b=b, k=k, sv=sv, c0=c0, c1=c1, nn=nn: e.matmul(
                            out=self.bank(b, nn), lhsT=sv[:, k, m * 128:(m + 1) * 128], rhs=self.hT[:, k, c0:c1],
                            start=(k == 0), stop=(k == 7)),
                          reads=rd, writes=[("ps", b)])
                ut, uk = self.tmp512()
                E("act", lambda e, ut=ut, b_u=b_u, nn=nn: e.activation(out=ut[:, 0:nn], in_=self.bank(b_u, nn), func=AF.Copy),
                  reads=[("ps", b_u)], writes=[uk])
                bt_, bk_ = self.tmp512()
                E("act", lambda e, bt_=bt_, b_b=b_b, nn=nn: e.activation(out=bt_[:, 0:nn], in_=self.bank(b_b, nn), func=AF.Copy),
                  reads=[("ps", b_b)], writes=[bk_])
                E("dve", lambda e, cu=cu, ut=ut, b_c=b_c, c0=c0, c1=c1, nn=nn: e.tensor_tensor(
                    out=cu[:, 2 + c0:2 + c1], in0=self.bank(b_c, nn), in1=ut[:, 0:nn], op=ALU.mult),
                  reads=[("ps", b_c), uk, ("cu", ci)], writes=[("cu", ci)])
                if g == 0 and ich == 0:
                    E("dve", lambda e, cu=cu: e.tensor_scalar(out=cu[:, 2 + 126:2 + 128], in0=cu[:, 2 + 126:2 + 128],
                                                              scalar1=self.flag[:, 0:1], scalar2=None, op0=ALU.mult),
                      reads=[("cu", ci), "flag"], writes=[("cu", ci)])
                at, ak = self.tmp512()
                E("dve", lambda e, at=at, cu=cu, j=j, c0=c0, c1=c1, nn=nn: e.tensor_scalar(
                    out=at[:, 0:nn], in0=cu[:, 2 + c0:2 + c1], scalar1=self.convw[:, j * 3 + 2:j * 3 + 3], scalar2=None,
                    op0=ALU.mult), reads=[("cu", ci), "convw"], writes=[ak])
                for kk, off in ((1, 1), (0, 0)):
                    E("dve", lambda e, at=at, cu=cu, j=j, c0=c0, c1=c1, nn=nn, kk=kk, off=off: e.scalar_tensor_tensor(
                        out=at[:, 0:nn], in0=cu[:, off + c0:off + c1], scalar=self.convw[:, j * 3 + kk:j * 3 + kk + 1],
                        in1=at[:, 0:nn], op0=ALU.mult, op1=ALU.add), reads=[("cu", ci), ak, "convw"], writes=[ak])
                E("dve", lambda e, at=at, bt_=bt_, j=j, c0=c0, c1=c1, nn=nn: e.tensor_tensor(
                    out=Y[:, j, c0:c1], in0=bt_[:, 0:nn], in1=at[:, 0:nn], op=ALU.mult),
                  reads=[bk_, ak], writes=[("R", j, t) for t in self.slots_of(c0, c1)])
            E("dve", lambda e, cu=cu, j=j: e.tensor_copy(out=self.cucarry[:, j, :], in_=cu[:, ns * 128:ns * 128 + 2]),
              reads=[("cu", ci)], writes=[("cucarry", j)])
        self.gated_proj(self.d_wcp, "wcp", chunks_1, first=True)
        for c in range(8):
            slot, key = self.ring_load(self.d_wq[c], 8 * 256)
            sv = slot[:, 0:2048].rearrange("p (k c) -> p k c", k=8)
            for (c0, c1) in chunks_1:
                nn = c1 - c0
                bq, bs = self.rr("psQ", [(0, 1), (2, 3)])
                rd = [key] + [("hT", t) for t in self.slots_of(c0, c1)]
                for m, b in ((0, bq), (1, bs)):
                    for k in range(8):
                        E("pe", lambda e, m=m, b=b, k=k, sv=sv, c0=c0, c1=c1, nn=nn: e.matmul(
                            out=self.bank(b, nn), lhsT=sv[:, k, m * 128:(m + 1) * 128], rhs=self.hT[:, k, c0:c1],
                            start=(k == 0), stop=(k == 7)), reads=rd, writes=[("ps", b)])
                t1, k1, t2, k2 = self.rope_evac(bq, bs, None, c0, c1)
                E("dve", lambda e, t1=t1, t2=t2, c=c, c0=c0, c1=c1, nn=nn: e.tensor_tensor(
                    out=Y[:, c, c0:c1], in0=t1[:, 0:nn], in1=t2[:, 0:nn], op=ALU.add),
                  reads=[k1, k2], writes=[("R", c, t) for t in self.slots_of(c0, c1)])
        for kg in range(4):
            slot, key = self.ring_load(self.d_wk[kg], 8 * 256)
            sv = slot[:, 0:2048].rearrange("p (k c) -> p k c", k=8)
            for (c0, c1) in chunks_lo:
                nn = c1 - c0
                bq, bs = self.rr("psQ", [(0, 1), (2, 3)])
                rd = [key] + [("hT", t) for t in self.slots_of(c0, c1)]
                for m, b in ((0, bq), (1, bs)):
                    for k in range(8):
                        E("pe", lambda e, m=m, b=b, k=k, sv=sv, c0=c0, c1=c1, nn=nn: e.matmul(
                            out=self.bank(b, nn), lhsT=sv[:, k, m * 128:(m + 1) * 128], rhs=self.hT[:, k, c0:c1],
                            start=(k == 0), stop=(k == 7)), reads=rd, writes=[("ps", b)])
                t1, k1, t2, k2 = self.rope_evac(bq, bs, None, c0, c1)
                E("dve", lambda e, t1=t1, t2=t2, kg=kg, c0=c0, c1=c1, nn=nn: e.tensor_tensor(
                    out=self.kT[:, kg, c0:c1], in0=t1[:, 0:nn], in1=t2[:, 0:nn], op=ALU.add),
                  reads=[k1, k2], writes=[("kT", kg, t) for t in self.slots_of(c0, c1)])
        slot, key = self.ring_load(self.d_wv[0], 8 * 256)
        sv = slot[:, 0:2048].rearrange("p (k c) -> p k c", k=8)
        for t in range(lo, ns):
            b = self.rr("psV", [0, 1, 2, 3])
            for k in range(8):
                E("pe", lambda e, k=k, b=b, t=t, sv=sv: e.matmul(
                    out=self.bank(b, 256), lhsT=self.hT[:, k, t * 128:(t + 1) * 128], rhs=sv[:, k, :],
                    start=(k == 0), stop=(k == 7)), reads=[key, ("hT", t)], writes=[("ps", b)])
            E("act", lambda e, b=b, t=t: e.activation(out=self.vv[:, t, :], in_=self.bank(b, 256), func=AF.Copy),
              reads=[("ps", b)], writes=[("v", t)])
        if cfg.stop == 3 and getattr(cfg, "sub", 9) < 1:
            return
        self.prefetch([(self.d_wap[j_], 8 * 256, ("wap", j_)) for j_ in range(3)])
        self.attention(g, ns)
        if cfg.stop == 3 and getattr(cfg, "sub", 9) < 2:
            return
        self.gated_proj(self.d_wap, "wap", chunks_1, first=False)
        wkeys = [("wdh", p) for p in range(3)]
        self.prefetch([(self.d_wgu[1][j_], 8 * 256, ("wgu", 1, j_)) for j_ in range(3)])
        if after_tile is not None:
            self.zero_ss()
        for t in range(1, ns):
            b0, b1 = self.rr("psA6", [(4, 5), (6, 7)])
            for k in range(8):
                for oh, b in ((0, b0), (1, b1)):
                    E("pe", lambda e, k=k, oh=oh, b=b, t=t: e.matmul(
                        out=self.bank(b), lhsT=self.R[:, MB + k, t * 128:(t + 1) * 128],
                        rhs=self.wdh[:, k, oh * 512:(oh + 1) * 512], start=(k == 0), stop=(k == 7)),
                      reads=wkeys + [("R", MB + k, t)], writes=[("ps", b)])
            for oh, b in ((0, b0), (1, b1)):
                tp, tk = self.tmp512()
                E("dve", lambda e, tp=tp, b=b, oh=oh: e.tensor_tensor(
                    out=tp, in0=self.bank(b), in1=self.bc[:, 1, oh * 512:(oh + 1) * 512], op=ALU.mult),
                  reads=[("ps", b), ("bc", 1)], writes=[tk])
                E("dve", lambda e, tp=tp, oh=oh, t=t: e.tensor_tensor(
                    out=self.xres[:, t, oh * 512:(oh + 1) * 512], in0=self.xres[:, t, oh * 512:(oh + 1) * 512],
                    in1=tp, op=ALU.add), reads=[tk, ("x", t)], writes=[("x", t)])
            if after_tile is not None:
                after_tile(t)

    def gated_proj(self, dsrc, dtag, chunks, first):
        E = self.E
        Y = self.R
        MB = 8
        for j in range(8):
            slot, key = self.ring_load(dsrc[j], 8 * 256, (dtag, j))
            sv = slot[:, 0:2048].rearrange("p (k c) -> p k c", k=8)
            for (c0, c1) in chunks:
                nn = c1 - c0
                by, bz = self.rr("psQ", [(0, 1), (2, 3)])
                sl = self.slots_of(c0, c1)
                for k in range(8):
                    E("pe", lambda e, k=k, by=by, sv=sv, c0=c0, c1=c1, nn=nn: e.matmul(
                        out=self.bank(by, nn), lhsT=sv[:, k, 0:128], rhs=Y[:, k, c0:c1], start=(k == 0), stop=(k == 7)),
                      reads=[key] + [("R", k, t) for t in sl], writes=[("ps", by)])
                for k in range(8):
                    E("pe", lambda e, k=k, bz=bz, sv=sv, c0=c0, c1=c1, nn=nn: e.matmul(
                        out=self.bank(bz, nn), lhsT=sv[:, k, 128:256], rhs=self.hT[:, k, c0:c1], start=(k == 0), stop=(k == 7)),
                      reads=[key] + [("hT", t) for t in sl], writes=[("ps", bz)])
                sg, sk = self.tmp512()
                E("act", lambda e, sg=sg, bz=bz, nn=nn: e.activation(out=sg[:, 0:nn], in_=self.bank(bz, nn), func=AF.Sigmoid),
                  reads=[("ps", bz)], writes=[sk])
                mk = [("R", MB + j, t) for t in sl]
                if first:
                    E("dve", lambda e, sg=sg, by=by, j=j, c0=c0, c1=c1, nn=nn: e.tensor_tensor(
                        out=self.R[:, MB + j, c0:c1], in0=self.bank(by, nn), in1=sg[:, 0:nn], op=ALU.mult),
                      reads=[("ps", by), sk], writes=mk)
                else:
                    E("dve", lambda e, sg=sg, by=by, nn=nn: e.tensor_tensor(
                        out=sg[:, 0:nn], in0=self.bank(by, nn), in1=sg[:, 0:nn], op=ALU.mult),
                      reads=[("ps", by), sk], writes=[sk])
                    E("dve", lambda e, sg=sg, j=j, c0=c0, c1=c1, nn=nn: e.tensor_tensor(
                        out=self.R[:, MB + j, c0:c1], in0=self.R[:, MB + j, c0:c1], in1=sg[:, 0:nn], op=ALU.add),
                      reads=[sk] + mk, writes=mk)

    def attention(self, g, ns):
        E = self.E
        Y = self.R
        yb = (6, 7)
        units = [(t, kg) for t in range(1, ns) for kg in range(4)]
        st = {}

        def tile_state(t):
            if t not in st:
                par = self.rr("astp", [0, 1])
                st[t] = dict(par=par, negm=self.ast[:, par, 0, :], rsum=self.ast[:, par, 1, :],
                             atmp=self.ast[:, par, 2, :], rden=self.ast[:, par, 3, :], akey=("ast", par))
                E("dve", lambda e, r=st[t]["rsum"]: e.memset(r, 0.0), writes=[("rsum", par, k_) for k_ in range(4)])
            return st[t]

        ust = {}

        def S1(u):
            t, kg = units[u]
            tile_state(t)
            b0, b1 = self.rr("psS", [(0, 1), (2, 3)])
            si = self.rr("Sm", [0, 1])
            ust[u] = dict(b0=b0, b1=b1, si=si)
            tc0, tc1 = t * 128, (t + 1) * 128
            for hh in range(4):
                ch, ph = 2 * kg + (hh % 2), (hh // 2) * 64
                b = b0 if hh < 2 else b1
                for kb in range(2):
                    o0 = (hh % 2) * 256 + kb * 128
                    ks = t - 1 + kb
                    E("pe", lambda e, b=b, o0=o0, ch=ch, ph=ph, ks=ks, kg=kg, tc0=tc0, tc1=tc1: e.matmul(
                        out=self.ps[:, b * 512 + o0:b * 512 + o0 + 128], lhsT=Y[ph:ph + 64, ch, tc0:tc1],
                        rhs=self.kT[ph:ph + 64, kg, ks * 128:(ks + 1) * 128], start=True, stop=True),
                      reads=[("R", ch, t), ("kT", kg, ks)], writes=[("ps", b)])

        def S2(u):
            t, kg = units[u]
            ts_ = st[t]
            negm, rsum, par = ts_["negm"], ts_["rsum"], ts_["par"]
            nk, rk = ("negm", par, kg), ("rsum", par, kg)
            b0, b1, si = ust[u]["b0"], ust[u]["b1"], ust[u]["si"]
            mi = 0 if (g == 0 and t == 1) else 1
            for half, b in ((0, b0), (1, b1)):
                E("dve", lambda e, half=half, b=b, si=si, mi=mi: e.scalar_tensor_tensor(
                    out=self.Sm[:, si, 2 * half:2 * half + 2, :],
                    in0=self.bank(b).rearrange("p (a b) -> p a b", a=2), scalar=HD ** -0.5,
                    in1=self.masks[:, mi:mi + 1, :].to_broadcast([128, 2, 256]), op0=ALU.mult, op1=ALU.add),
                  reads=[("ps", b), "masks"], writes=[("Sm", si, half)])
            E("dve", lambda e, si=si, kg=kg, negm=negm: e.tensor_reduce(out=negm[:, 4 * kg:4 * kg + 4], in_=self.Sm[:, si, :, :],
                                                                        axis=AX.X, op=ALU.max),
              reads=[("Sm", si, 0), ("Sm", si, 1)], writes=[nk])
            E("dve", lambda e, kg=kg, negm=negm: e.scalar_tensor_tensor(
                out=negm[:, 4 * kg:4 * kg + 4], in0=negm[:, 4 * kg:4 * kg + 4], scalar=-1.0,
                in1=self.sinkbc[:, 1, 4 * kg:4 * kg + 4], op0=ALU.mult, op1=ALU.min),
              reads=[nk, "sink1"], writes=[nk])
            for hh in range(4):
                h = 4 * kg + hh
                E("act", lambda e, si=si, hh=hh, h=h, negm=negm, rsum=rsum: e.activation(
                    out=self.P[:, si, hh, :], in_=self.Sm[:, si, hh, :], func=AF.Exp, bias=negm[:, h:h + 1],
                    accum_out=rsum[:, h:h + 1]),
                  reads=[("Sm", si, hh // 2), nk, rk], writes=[("P", si, hh), rk])

        def S3a(u):
            si = ust[u]["si"]
            bt = self.rr("psPT", [4, 5])
            ptv = self.bank_bf(bt)
            for hh in range(4):
                for kb in range(2):
                    o = (hh * 2 + kb) * 128
                    E("pe", lambda e, si=si, hh=hh, kb=kb, o=o, ptv=ptv: e.transpose(
                        out=ptv[:, o:o + 128], in_=self.P[:, si, hh, kb * 128:(kb + 1) * 128], identity=self.idb[:]),
                      reads=[("P", si, hh), "idb"], writes=[("ps", bt)])
            if u % 2 == 0:
                E("dve", lambda e, si=si, ptv=ptv: e.tensor_copy(out=self.pT[:, si, :], in_=ptv),
                  reads=[("ps", bt)], writes=[("pT", si)])
            else:
                E("act", lambda e, si=si, ptv=ptv: e.activation(out=self.pT[:, si, :], in_=ptv, func=AF.Copy),
                  reads=[("ps", bt)], writes=[("pT", si)])

        def S3b(u):
            t, kg = units[u]
            si = ust[u]["si"]
            for hh in range(4):
                h = 4 * kg + hh
                b = yb[h // 8]
                for kb in range(2):
                    o = (hh * 2 + kb) * 128
                    ks = t - 1 + kb
                    E("pe", lambda e, si=si, o=o, ks=ks, kg=kg, h=h, kb=kb, b=b: e.matmul(
                        out=self.ps[:, b * 512 + (h % 8) * 64:b * 512 + (h % 8 + 1) * 64], lhsT=self.pT[:, si, o:o + 128],
                        rhs=self.vv[:, ks, kg * 64:(kg + 1) * 64], start=(kb == 0), stop=(kb == 1)),
                      reads=[("pT", si), ("v", ks)], writes=[("ps", b)])
            if kg == 3:
                tile_end_a1(t)

        def tile_end_a1(t):
            ts_ = st[t]
            negm, atmp, par = ts_["negm"], ts_["atmp"], ts_["par"]
            nks = [("negm", par, k_) for k_ in range(4)]
            tk_ = ("atmp", par)
            E("dve", lambda e: e.tensor_tensor(out=atmp, in0=negm, in1=self.sinkbc[:, 0, :], op=ALU.add),
              reads=nks + ["sink0"], writes=[tk_])
            E("act", lambda e: e.activation(out=atmp, in_=atmp, func=AF.Exp), reads=[tk_], writes=[tk_])

        def tile_end_a2(t):
            ts_ = st[t]
            negm, rsum, atmp, rden, par = ts_["negm"], ts_["rsum"], ts_["atmp"], ts_["rden"], ts_["par"]
            nks = [("negm", par, k_) for k_ in range(4)]
            rks = [("rsum", par, k_) for k_ in range(4)]
            tk_, dk_ = ("atmp", par), ("rden", par)
            E("dve", lambda e: e.tensor_tensor(out=atmp, in0=atmp, in1=rsum, op=ALU.add), reads=[tk_] + rks, writes=[tk_])
            E("dve", lambda e: e.reciprocal(out=rden, in_=atmp), reads=[tk_], writes=[dk_])
            for half in range(2):
                E("dve", lambda e, half=half: e.tensor_tensor(
                    out=self.ybf[:, half * 512:(half + 1) * 512].rearrange("p (a b) -> p a b", a=8),
                    in0=self.bank(yb[half]).rearrange("p (a b) -> p a b", a=8),
                    in1=rden[:, 8 * half:8 * half + 8].unsqueeze(2).to_broadcast([128, 8, 64]), op=ALU.mult),
                  reads=[("ps", yb[half]), dk_], writes=[("ybf", half)])

        def tile_end_b(t):
            tc0, tc1 = t * 128, (t + 1) * 128
            bt = self.rr("psPT", [4, 5])
            ytv = self.bank_bf(bt)
            for k in range(8):
                E("pe", lambda e, k=k, ytv=ytv: e.transpose(out=ytv[:, k * 128:(k + 1) * 128],
                                                            in_=self.ybf[:, k * 128:(k + 1) * 128], identity=self.idb[:]),
                  reads=[("ybf", k // 4), "idb"], writes=[("ps", bt)])
            E("act", lambda e, ytv=ytv: e.activation(out=Y[:, 0:8, tc0:tc1], in_=ytv.rearrange("p (a b) -> p a b", a=8),
                                                     func=AF.Copy),
              reads=[("ps", bt)], writes=[("R", k, t) for k in range(8)])

        n = len(units)
        for s_ in range(n + 6):
            if s_ < n:
                S1(s_)
            if 0 <= s_ - 1 < n:
                S2(s_ - 1)
            if 0 <= s_ - 2 < n:
                S3a(s_ - 2)
            if 0 <= s_ - 4 < n and units[s_ - 4][1] == 3:
                tile_end_a2(units[s_ - 4][0])
            if 0 <= s_ - 3 < n:
                S3b(s_ - 3)
            if 0 <= s_ - 5 < n and units[s_ - 5][1] == 3:
                tile_end_b(units[s_ - 5][0])

    def final_tile(self, g, t):
        E = self.E
        tile0 = sum(self.cfg.groups[:g])
        oi = self.rr("nt", [0, 1])
        fc = self.cfg.nsmax + t
        self.rstd_tile(t, self.nt[:, oi, :].bitcast(BF16)[:, 0:D], ("nt", oi), col=fc)
        E("dve", lambda e: e.scalar_tensor_tensor(
            out=self.nt[:, oi, :], in0=self.xres[:, t, :], scalar=self.stat[:, 1, fc:fc + 1], in1=self.bc[:, 3, :],
            op0=ALU.mult, op1=ALU.mult), reads=[("x", t), ("rstd", fc), ("bc", 3)], writes=[("nt", oi)])
        E("sp", lambda e: e.dma_start(out=self.d_out[tile0 + t - 1], in_=self.nt[:, oi, :]),
          reads=[("nt", oi)], writes=[("out", tile0 + t - 1)], dma_sem="st%d" % oi)

    def load_x_tile(self, g, t):
        tile0 = sum(self.cfg.groups[:g])
        self.E("sp", lambda e: e.dma_start(out=self.xres[:, t, :], in_=self.d_x[tile0 + t]),
               writes=[("x", t)], dma_sem="x%d" % t)

    def build(self):
        nc, cfg = self.nc, self.cfg
        self.declare_dram()
        with ExitStack() as es:
            es.enter_context(nc.allow_low_precision("bf16 matmul operands, fp32 accumulation"))
            self.alloc(es)
            self.setup()
            ng = len(cfg.groups)
            for t in range(0, cfg.groups[0] + 1):
                self.load_x_tile(0, t)
            for l in range(8):
                self.ada_load(l)
            self.ada_finish(0)
            self.zero_ss()
            for t in range(0, cfg.groups[0] + 1):
                self.norm_tile(t, 0, banks=(4, 5, 6))
            pending = list(range(8, N_ADA_LOADS))

            def extra(upto=None):
                while pending:
                    l = pending.pop(0)
                    self.ada_load(l)
                    if l == 15:
                        self.ada_finish(1)
                    if l == 23:
                        self.ada_finish(2)
                    if upto is None or l >= upto:
                        break

            ens_prev = None
            for g in range(ng):
                lo = 0 if g == 0 else 1
                ns = cfg.groups[g] + 1
                at, ens1, fl = self.norm_pipe(1, lazy_from=(5 if g > 0 else 4))
                self.ffn(g, 0, lo, extra=extra if g == 0 else None, after_tile=at, ensure=ens_prev,
                         next_loads=[(self.d_wcv[j_], 8 * 384, ("wcv", j_)) for j_ in range(3)])
                while pending:
                    extra()
                at, ens2, fl2 = self.norm_pipe(2, lazy_from=5)
                self.mixer(g, after_tile=at, ensure=ens1)
                fl()
                nns = cfg.groups[g + 1] + 1 if g + 1 < ng else 0
                at0, ens0, fl0 = self.norm_pipe(0, delay=1, lazy_from=5)

                xl_pend = []

                def after_ffn2(t, g=g, nns=nns, at0=at0, xl_pend=xl_pend):
                    self.final_tile(g, t)
                    if t < nns:
                        self.load_x_tile(g + 1, t)
                        xl_pend.append(t)
                        if len(xl_pend) > 1:
                            at0(xl_pend.pop(0))

                self.ffn(g, 1, 1, after_tile=after_ffn2, ensure=ens2,
                         next_loads=[(self.d_wgu[0][j_], 8 * 256, ("wgu", 0, j_)) for j_ in range(3)] if g + 1 < ng else None)
                fl2()
                while xl_pend:
                    at0(xl_pend.pop(0))
                for t in range(ns, nns):
                    self.load_x_tile(g + 1, t)
                    at0(t)
                ens_prev = ens0
            self.E("sp", None, reads=[("out", i) for i in range(cfg.ntc)])
            streams = self.tr.finalize()
            sems = {}
            for e_ in Tracker.ENGS:
                sems[("E", e_)] = es.enter_context(nc.semaphore("s_" + e_))
            for k in self.tr.dma_sems:
                sems[("D", k)] = es.enter_context(nc.semaphore("d_" + k))
            block = es.enter_context(nc.Block())

            def replay(eng, name):
                mysem = sems[("E", name)]
                for item in streams[name]:
                    if item[0] == "wait":
                        eng.wait_ge(sems[item[1]], item[2])
                    else:
                        op = item[1]
                        ins = op.fn(eng)
                        if op.dma_sem is not None:
                            ins.then_inc(sems[("D", op.dma_sem)], 16)
                        elif op.needed:
                            ins.then_inc(mysem, 1)

            @block.tensor
            def _(e):
                replay(e, "pe")

            @block.vector
            def _(e):
                replay(e, "dve")

            @block.scalar
            def _(e):
                replay(e, "act")

            @block.gpsimd
            def _(e):
                replay(e, "pool")

            @block.sync
            def _(e):
                replay(e, "sp")
        return nc


def _pack(W, cols):
    K = W.shape[0]
    Wc = W[:, cols]
    return np.ascontiguousarray(Wc.reshape(K // 128, 128, len(cols)).transpose(1, 0, 2)).reshape(128, -1)


def _col(v):
    return np.ascontiguousarray(v.reshape(-1, 128).T)


def prepare_weights(inp):
    f = lambda a: np.asarray(a, dtype=np.float32)
    ar = np.arange
    w_ada = f(inp["w_ada"])[0]
    w_in = f(inp["w_in"])[0]
    out = {}
    out["wada"] = np.stack([_pack(w_ada, ar(l * 384, (l + 1) * 384)) for l in range(N_ADA_LOADS)])
    for fi, (gu, dn) in enumerate((("w1_gu", "w1_down"), ("w2_gu", "w2_down"))):
        wgu = f(inp[gu])[0]
        wd = f(inp[dn])[0]
        out["wgu%d" % fi] = np.stack([_pack(wgu, np.concatenate([ar(j * 128, (j + 1) * 128), DFF + ar(j * 128, (j + 1) * 128)]))
                                      for j in range(NJ)])
        out["wdh%d" % fi] = np.stack([_pack(wd[hh * NJH * 128:(hh + 1) * NJH * 128], ar(D)) for hh in range(2)])
    out["wcv"] = np.stack([_pack(w_in, np.concatenate([O_B + ar(j * 128, (j + 1) * 128), O_C + ar(j * 128, (j + 1) * 128),
                                                        O_U + ar(j * 128, (j + 1) * 128)])) for j in range(8)])

    def swp(base):
        return np.concatenate([base + ar(32, 64), base + ar(0, 32)])

    def qheads(c):
        g_, e_ = c // 2, c % 2
        return 4 * g_ + e_, 4 * g_ + 2 + e_

    out["wq"] = np.stack([_pack(w_in, np.concatenate([O_Q + qheads(c)[0] * 64 + ar(64), O_Q + qheads(c)[1] * 64 + ar(64),
                                                       swp(O_Q + qheads(c)[0] * 64), swp(O_Q + qheads(c)[1] * 64)]))
                          for c in range(8)])
    out["wk"] = np.stack([_pack(w_in, np.concatenate([O_K + g * 64 + ar(64), O_K + g * 64 + ar(64),
                                                       swp(O_K + g * 64), swp(O_K + g * 64)])) for g in range(4)])
    out["wv"] = _pack(w_in, O_V + ar(256))[None]
    wcp = f(inp["w_conv_proj"])[0]
    wap = f(inp["w_attn_proj"])[0]
    out["wcp"] = np.stack([np.concatenate([_pack(wcp, ar(j * 128, (j + 1) * 128)).reshape(128, 8, 128),
                                            _pack(w_in, O_ZC + ar(j * 128, (j + 1) * 128)).reshape(128, 8, 128)],
                                           axis=2).reshape(128, -1) for j in range(8)])
    out["wap"] = np.stack([np.concatenate([_pack(wap, ar(j * 128, (j + 1) * 128)).reshape(128, 8, 128),
                                            _pack(w_in, O_ZA + ar(j * 128, (j + 1) * 128)).reshape(128, 8, 128)],
                                           axis=2).reshape(128, -1) for j in range(8)])
    out["wout"] = _pack(f(inp["w_out"])[0], ar(D))
    out["gcol"] = np.concatenate([_col(f(inp[n])[0]) for n in ("g_ffn1", "g_mix", "g_ffn2")], axis=1)
    out["badacol"] = _col(f(inp["b_ada"])[0])
    cw = f(inp["conv_w"])[0]
    out["convw"] = np.ascontiguousarray(cw.reshape(3, 8, 128).transpose(2, 1, 0)).reshape(128, 24)
    out["gfin"] = f(inp["g_final"])
    out["sinks"] = f(inp["sinks"])[0]
    out["ident"] = np.eye(128, dtype=np.float32)
    return out


def rope_tables_T(pos):
    inv = (1.0 / (ROPE_THETA ** (np.arange(0, HD, 2, dtype=np.float32) / np.float32(HD)))).astype(np.float32)
    ang = (pos.astype(np.float32)[None, :] * inv[:, None]).astype(np.float32)
    cos = np.cos(ang).astype(np.float32)
    sin = np.sin(ang).astype(np.float32)
    cosT = np.tile(cos, (4, 1))
    sgn = np.where((np.arange(128) % 64) < 32, -1.0, 1.0).astype(np.float32)
    sinT = np.tile(sin, (4, 1)) * sgn[:, None]
    return np.ascontiguousarray(cosT), np.ascontiguousarray(sinT)


def band_masks(first_half):
    qi = np.arange(128)[:, None]
    kj = np.arange(256)[None, :]
    diff = 128 + qi - kj
    valid = (diff >= 0) & (diff < 128)
    reg = np.where(valid, 0.0, NEG).astype(np.float32)
    m0 = reg.copy()
    if first_half:
        m0[:, 0:128] = NEG
    return np.ascontiguousarray(np.concatenate([m0, reg], axis=1))


def core_inputs(cfg, wts, x, c, b, s):
    n = cfg.ntc * 128
    xin = np.zeros((cfg.ntc + 1, 128, D), dtype=np.float32)
    xin[1:] = x[b, s * n:(s + 1) * n].reshape(cfg.ntc, 128, D)
    if s > 0:
        xin[0] = x[b, s * n - 128:s * n]
    pos = np.arange(s * n - 128, (s + 1) * n)
    cosT, sinT = rope_tables_T(pos)
    m = dict(wts)
    m["xin"] = xin
    m["ccol"] = _col(c[b])
    m["masks"] = band_masks(s == 0)
    m["flag"] = np.full((128, 1), 0.0 if s == 0 else 1.0, dtype=np.float32)
    m["cost"] = cosT
    m["sint"] = sinT
    return m


_NC_CACHE = {}


def run(cfg, inputs, nsplit, trace=False):
    x = np.asarray(inputs["x"], dtype=np.float32)
    c = np.asarray(inputs["c"], dtype=np.float32)
    B, S, _ = x.shape
    assert S == nsplit * cfg.ntc * 128
    wts = prepare_weights(inputs)
    in_maps = []
    for b in range(B):
        for s in range(nsplit):
            in_maps.append(core_inputs(cfg, wts, x, c, b, s))
    key = (cfg.ntc, tuple(cfg.groups), cfg.nring)
    if key not in _NC_CACHE:
        _NC_CACHE[key] = Builder(cfg).build()
    nc = _NC_CACHE[key]
    res = run_bass_kernel_spmd(nc, in_maps, core_ids=list(range(len(in_maps))), trace=trace)
    out = np.empty((B, S, D), dtype=np.float32)
    n = cfg.ntc * 128
    i = 0
    for b in range(B):
        for s in range(nsplit):
            out[b, s * n:(s + 1) * n] = np.asarray(res.results[i]["out"]).reshape(n, D)
            i += 1
    return out, res


def kernel(**inputs):
    cfg = Cfg(ntc=32, groups=(7, 7, 6, 6, 6), nring=3, ntmp=4)
    out, _ = run(cfg, inputs, nsplit=2)
    return out
```

```python
from contextlib import ExitStack

import numpy as np
import concourse.bass as bass
import concourse.mybir as mybir
from concourse.bass_utils import run_bass_kernel_spmd

F32 = mybir.dt.float32
BF16 = mybir.dt.bfloat16
ALU = mybir.AluOpType
AF = mybir.ActivationFunctionType
AX = mybir.AxisListType

D = 1024
DFF = 2816
NJ = 22
NJH = 11
HD = 64
NH = 16
NKV = 4
EPS = 1e-6
NEG = -1e30
ROPE_THETA = 10000.0
RING_ELEMS = 3072
N_ADA_LOADS = 24

O_B, O_C, O_U, O_Q, O_K, O_V, O_ZC, O_ZA = 0, 1024, 2048, 3072, 4096, 4352, 4608, 5632


class Cfg:
    def __init__(self, ntc=32, groups=(8, 8, 8, 8), nring=4, ntmp=8, stop=99):
        assert sum(groups) == ntc
        self.stop = stop
        self.ntc = ntc
        self.groups = list(groups)
        self.nsmax = max(groups) + 1
        self.ncmax = self.nsmax * 128
        self.nring = nring
        self.ntmp = ntmp


class _Op:
    __slots__ = ("eng", "fn", "deps", "dma_sem", "sigval", "needed", "group")

    def __init__(self, eng, fn, deps, dma_sem, group=None):
        self.group = group
        self.eng = eng
        self.fn = fn
        self.deps = deps
        self.dma_sem = dma_sem
        self.sigval = 0
        self.needed = False


class Tracker:
    ENGS = ("pe", "act", "dve", "pool", "sp")

    def __init__(self):
        self.ops = []
        self.last_w = {}
        self.rd_eng = {}
        self.rd_dma = {}

    def emit(self, eng, fn, reads=(), writes=(), dma_sem=None, group=None):
        i = len(self.ops)
        deps = set()
        for r in reads:
            w = self.last_w.get(r)
            if w is not None:
                deps.add(w)
        for wr in writes:
            w = self.last_w.get(wr)
            if w is not None:
                deps.add(w)
            for d in self.rd_eng.get(wr, {}).values():
                deps.add(d)
            for d in self.rd_dma.get(wr, ()):
                deps.add(d)
        if eng == "pe":
            deps = {d for d in deps if not (self.ops[d].eng == "pe" and self.ops[d].dma_sem is None)}
        if group is not None:
            deps = {d for d in deps if not (self.ops[d].group == group and self.ops[d].dma_sem == dma_sem)}
        self.ops.append(_Op(eng, fn, deps, dma_sem, group))
        for r in reads:
            if dma_sem is not None:
                self.rd_dma.setdefault(r, []).append(i)
            else:
                self.rd_eng.setdefault(r, {})[eng] = i
        for wr in writes:
            self.last_w[wr] = i
            self.rd_eng[wr] = {}
            self.rd_dma[wr] = []
        return i

    def finalize(self):
        ops = self.ops
        for op in ops:
            for d in op.deps:
                ops[d].needed = True
        cnt = {e: 0 for e in self.ENGS}
        dcnt = {}
        for op in ops:
            if op.dma_sem is not None:
                dcnt[op.dma_sem] = dcnt.get(op.dma_sem, 0) + 16
                op.sigval = dcnt[op.dma_sem]
            elif op.needed:
                cnt[op.eng] += 1
                op.sigval = cnt[op.eng]
        gmax = {}
        for op in ops:
            if op.group is not None:
                gk = (op.dma_sem, op.group)
                gmax[gk] = max(gmax.get(gk, 0), op.sigval)
        for op in ops:
            if op.group is not None:
                op.sigval = gmax[(op.dma_sem, op.group)]
        streams = {e: [] for e in self.ENGS}
        known = {e: {} for e in self.ENGS}
        nwait = 0
        for op in ops:
            waits = {}
            for d in op.deps:
                dop = ops[d]
                key = ("D", dop.dma_sem) if dop.dma_sem is not None else ("E", dop.eng)
                if waits.get(key, 0) < dop.sigval:
                    waits[key] = dop.sigval
            kn = known[op.eng]
            for key, val in waits.items():
                if kn.get(key, 0) < val:
                    streams[op.eng].append(("wait", key, val))
                    kn[key] = val
                    nwait += 1
            if op.fn is not None:
                streams[op.eng].append(("op", op))
        self.dma_sems = sorted(dcnt.keys())
        self.nwait = nwait
        return streams


class Builder:
    def __init__(self, cfg):
        self.cfg = cfg
        self.nc = bass.Bass("TRN2", target_bir_lowering=False)
        self.tr = Tracker()
        self.ring_i = 0
        self.ring_loads = 0
        self.tmp_i = 0
        self.ps_rr = {}
        self.gid = 0
        self.prefetched = []

    def declare_dram(self):
        nc, cfg = self.nc, self.cfg
        di = lambda n, s: nc.dram_tensor(n, list(s), F32, kind="ExternalInput").ap()
        self.d_x = di("xin", [cfg.ntc + 1, 128, D])
        self.d_ccol = di("ccol", [128, 8])
        self.d_gcol = di("gcol", [128, 24])
        self.d_bada = di("badacol", [128, 72])
        self.d_convw = di("convw", [128, 24])
        self.d_gfin = di("gfin", [D])
        self.d_sinks = di("sinks", [NH])
        self.d_ident = di("ident", [128, 128])
        self.d_masks = di("masks", [128, 512])
        self.d_flag = di("flag", [128, 1])
        self.d_cos = di("cost", [128, (cfg.ntc + 1) * 128])
        self.d_sin = di("sint", [128, (cfg.ntc + 1) * 128])
        self.d_wada = di("wada", [N_ADA_LOADS, 128, 8 * 384])
        self.d_wgu = [di("wgu%d" % f, [NJ, 128, 8 * 256]) for f in range(2)]
        self.d_wdh = [di("wdh%d" % f, [2, 128, NJH * 1024]) for f in range(2)]
        self.d_wcv = di("wcv", [8, 128, 8 * 384])
        self.d_wq = di("wq", [8, 128, 8 * 256])
        self.d_wk = di("wk", [4, 128, 8 * 256])
        self.d_wv = di("wv", [1, 128, 8 * 256])
        self.d_wcp = di("wcp", [8, 128, 8 * 256])
        self.d_wap = di("wap", [8, 128, 8 * 256])
        self.d_wout = di("wout", [128, 8 * 1024])
        self.d_out = nc.dram_tensor("out", [cfg.ntc, 128, D], F32, kind="ExternalOutput").ap()

    def alloc(self, es):
        nc, cfg = self.nc, self.cfg
        NS, NC = cfg.nsmax, cfg.ncmax
        sb = lambda n, s, dt: es.enter_context(nc.sbuf_tensor(n, list(s), dt))
        self.xres = sb("xres", [128, NS, D], F32)
        self.hT = sb("hT", [128, 8, NC], BF16)
        self.R = sb("R", [128, 16, NC], BF16)
        self.wdh = sb("wdhb", [128, NJH, 1024], BF16)
        self.kT = sb("kT", [128, 4, NC], BF16)
        self.vv = sb("vv", [128, NS, 256], BF16)
        self.ring = sb("ring", [128, cfg.nring, RING_ELEMS], BF16)
        self.bc = sb("bc", [128, 4, D], F32)
        self.cosT = sb("cosT", [128, NC], F32)
        self.sinT = sb("sinT", [128, NC], F32)
        self.tmp = sb("tmp", [128, cfg.ntmp, 512], F32)
        self.xn = sb("xn", [128, 4, D], BF16)
        self.nt = sb("nt", [128, 2, D], F32)
        self.cu = sb("cu", [128, 1, NC + 2], F32)
        self.cucarry = sb("cucarry", [128, 8, 2], F32)
        self.Sm = sb("Sm", [128, 2, 4, 256], F32)
        self.P = sb("P", [128, 2, 4, 256], BF16)
        self.pT = sb("pT", [128, 2, 1024], BF16)
        self.ybf = sb("ybf", [128, D], BF16)
        self.ast = sb("ast", [128, 2, 4, 16], F32)
        self.stat = sb("stat", [128, 2, 2 * NS], F32)
        self.epsc = sb("epsc", [128, 1], F32)
        self.idf = sb("idf", [128, 128], F32)
        self.idb = sb("idb", [128, 128], BF16)
        self.ones = sb("ones", [128, 128], F32)
        self.ccol = sb("ccols", [128, 8], F32)
        self.cbf = sb("cbf", [128, 8], BF16)
        self.gcol = sb("gcols", [128, 24], F32)
        self.bada = sb("badas", [128, 72], F32)
        self.modcol = sb("modcol", [128, 72], F32)
        self.gs = sb("gs", [128, 6, 8], F32)
        self.convw = sb("convws", [128, 24], F32)
        self.sinkbc = sb("sinkbc", [128, 2, NH], F32)
        self.masks = sb("maskss", [128, 2, 256], F32)
        self.flag = sb("flags", [128, 1], F32)
        self.ps = es.enter_context(nc.psum_tensor("ps", [128, 4096], F32))

    def bank(self, b, n=512):
        return self.ps[:, b * 512:b * 512 + n]

    def bank_bf(self, b):
        return self.ps[:, b * 512:(b + 1) * 512].bitcast(BF16)

    def rr(self, name, options):
        i = self.ps_rr.get(name, 0)
        self.ps_rr[name] = i + 1
        return options[i % len(options)]

    def tmp512(self):
        i = self.tmp_i % self.cfg.ntmp
        self.tmp_i += 1
        return self.tmp[:, i, :], ("tmp", i)

    def E(self, eng, fn, reads=(), writes=(), dma_sem=None, group=None):
        return self.tr.emit(eng, fn, reads, writes, dma_sem, group)

    def newgroup(self, bump=False):
        if bump:
            self.gid += 1
        return self.gid

    def _ring_issue(self, src, nelem):
        i = self.ring_i % self.cfg.nring
        self.ring_i += 1
        dst = self.ring[:, i, 0:nelem]
        self.E("pool", lambda e, dst=dst, src=src: e.dma_start(out=dst, in_=src),
               reads=(), writes=[("ring", i)], dma_sem="ring%d" % i)
        return self.ring[:, i, :], ("ring", i)

    def ring_load(self, src, nelem, tag=None):
        if self.prefetched:
            ptag, slot, key = self.prefetched.pop(0)
            assert ptag == tag and tag is not None, (ptag, tag)
            return slot, key
        return self._ring_issue(src, nelem)

    def prefetch(self, loads):
        assert not self.prefetched
        for src, nelem, tag in loads[:self.cfg.nring]:
            slot, key = self._ring_issue(src, nelem)
            self.prefetched.append((tag, slot, key))

    def nchunks(self, lo, ns):
        out = []
        c = lo * 128
        end = ns * 128
        while c < end:
            n = min(512, end - c)
            out.append((c, c + n))
            c += n
        return out

    @staticmethod
    def slots_of(c0, c1):
        return list(range(c0 // 128, (c1 + 127) // 128))

    def setup(self):
        E = self.E
        ld = lambda dst, src, w: E("sp", lambda e: e.dma_start(out=dst, in_=src), writes=[w], dma_sem="const", group=0)
        ld(self.ccol[:], self.d_ccol, "ccol")
        ld(self.bada[:], self.d_bada, "bada")
        ld(self.gcol[:], self.d_gcol, "gcol")
        ld(self.idf[:], self.d_ident, "idf")
        ld(self.convw[:], self.d_convw, "convw")
        ld(self.masks[:].rearrange("p a b -> p (a b)"), self.d_masks, "masks")
        ld(self.flag[:], self.d_flag, "flag")
        ld(self.sinkbc[:, 0, :], self.d_sinks.partition_broadcast(128), "sink0")
        ld(self.bc[:, 3, :], self.d_gfin.partition_broadcast(128), ("bc", 3))
        E("dve", lambda e: e.tensor_copy(out=self.idb[:], in_=self.idf[:]), reads=["idf"], writes=["idb"])
        E("dve", lambda e: e.memset(self.ones[:], 1.0), writes=["ones"])
        E("dve", lambda e: e.memset(self.epsc[:], EPS), writes=["epsc"])
        E("dve", lambda e: e.memset(self.cucarry[:], 0.0), writes=["cucarry"])
        E("dve", lambda e: e.tensor_scalar(out=self.sinkbc[:, 1, :], in0=self.sinkbc[:, 0, :], scalar1=-1.0,
                                           scalar2=None, op0=ALU.mult), reads=["sink0"], writes=["sink1"])
        E("act", lambda e: e.activation(out=self.ccol[:], in_=self.ccol[:], func=AF.Silu),
          reads=["ccol"], writes=["ccol"])
        E("dve", lambda e: e.tensor_copy(out=self.cbf[:], in_=self.ccol[:]), reads=["ccol"], writes=["cbf"])

    def ada_load(self, l):
        E = self.E
        slot, key = self.ring_load(self.d_wada[l], 8 * 384)
        sv = slot[:, 0:8 * 384].rearrange("p (k c) -> p k c", k=8)
        for mi in range(3):
            mc = 3 * l + mi
            for k in range(8):
                E("pe", lambda e, mc=mc, k=k, mi=mi, sv=sv: e.matmul(
                    out=self.ps[:, 7 * 512 + mc:7 * 512 + mc + 1], lhsT=sv[:, k, mi * 128:(mi + 1) * 128],
                    rhs=self.cbf[:, k:k + 1], start=(k == 0), stop=(k == 7)),
                  reads=[key, "cbf"], writes=[("ps", 7)])
        E("dve", lambda e: e.tensor_tensor(out=self.modcol[:, 3 * l:3 * l + 3], in0=self.ps[:, 7 * 512 + 3 * l:7 * 512 + 3 * l + 3],
                                           in1=self.bada[:, 3 * l:3 * l + 3], op=ALU.add),
          reads=[("ps", 7), "bada"], writes=[("modcol", l)])

    def ada_finish(self, n):
        E = self.E
        c0 = 24 * n
        mrd = [("modcol", l) for l in range(8 * n, 8 * n + 8)]
        E("dve", lambda e: e.tensor_copy(out=self.gs[:, 2 * n + 1, :], in_=self.modcol[:, c0:c0 + 8]),
          reads=mrd, writes=[("gs", 2 * n + 1)])
        E("dve", lambda e: e.scalar_tensor_tensor(out=self.gs[:, 2 * n, :], in0=self.modcol[:, c0 + 8:c0 + 16], scalar=1.0,
                                                  in1=self.gcol[:, 8 * n:8 * n + 8], op0=ALU.add, op1=ALU.mult),
          reads=mrd + ["gcol"], writes=[("gs", 2 * n)])
        gsc = 1.0 if n == 1 else 0.5
        for k in range(8):
            E("dve", lambda e, k=k: e.tensor_scalar(out=self.nt[:, 0, k * 128:(k + 1) * 128], in0=self.idf[:],
                                                    scalar1=self.modcol[:, c0 + 16 + k:c0 + 17 + k], scalar2=gsc,
                                                    op0=ALU.mult, op1=ALU.mult),
              reads=mrd + ["idf"], writes=[("nt", 0)])
        for half in range(2):
            b = 5 + half
            E("pe", lambda e, half=half, b=b: e.matmul(
                out=self.bank(b), lhsT=self.ones[:],
                rhs=self.nt[:, 0, half * 512:(half + 1) * 512], start=True, stop=True),
              reads=["ones", ("nt", 0)], writes=[("ps", b)])
            E("act", lambda e, half=half, b=b: e.activation(out=self.bc[:, n, half * 512:(half + 1) * 512],
                                                            in_=self.bank(b), func=AF.Copy),
              reads=[("ps", b)], writes=[("bc", n)])

    def zero_ss(self):
        nsc = 2 * self.cfg.nsmax
        self.E("dve", lambda e: e.memset(self.stat[:, 0, :], 0.0), writes=[("ss", c) for c in range(nsc)])

    def rstd_tile(self, t, junk, jkey, col=None):
        E = self.E
        col = t if col is None else col
        E("act", lambda e: e.activation(out=junk, in_=self.xres[:, t, :], func=AF.Square,
                                        accum_out=self.stat[:, 0, col:col + 1]),
          reads=[("x", t), ("ss", col)], writes=[("ss", col), jkey])
        E("act", lambda e: e.activation(out=self.stat[:, 1, col:col + 1], in_=self.stat[:, 0, col:col + 1], func=AF.Sqrt,
                                        bias=self.epsc[:, 0:1], scale=1.0 / D),
          reads=[("ss", col), "epsc"], writes=[("rstd", col)])
        E("dve", lambda e: e.reciprocal(out=self.stat[:, 1, col:col + 1], in_=self.stat[:, 1, col:col + 1]),
          reads=[("rstd", col)], writes=[("rstd", col)])

    def norm_a(self, t):
        E = self.E
        xi = self.rr("xn", [0, 1, 2, 3])
        self.rstd_tile(t, self.xn[:, xi, :], ("xn", xi))
        E("act", lambda e: e.activation(out=self.xn[:, xi, :], in_=self.xres[:, t, :], func=AF.Copy,
                                        scale=self.stat[:, 1, t:t + 1]),
          reads=[("x", t), ("rstd", t)], writes=[("xn", xi)])
        return xi

    def norm_b(self, t, n, xi, banks=(0, 1, 2, 3)):
        E = self.E
        bb = self.rr("psT", list(banks))
        pv = self.bank_bf(bb)
        for k in range(8):
            E("pe", lambda e, k=k: e.transpose(out=pv[:, k * 128:(k + 1) * 128], in_=self.xn[:, xi, k * 128:(k + 1) * 128],
                                               identity=self.idb[:]),
              reads=[("xn", xi), "idb"], writes=[("ps", bb)])
        ni = self.rr("nt", [0, 1])
        ntv = self.nt[:, ni, :].rearrange("p (k c) -> p k c", k=8)
        E("dve", lambda e: e.tensor_tensor(out=ntv, in0=pv.rearrange("p (k c) -> p k c", k=8),
                                           in1=self.gs[:, 2 * n, :].unsqueeze(2).to_broadcast([128, 8, 128]), op=ALU.mult),
          reads=[("ps", bb), ("gs", 2 * n)], writes=[("nt", ni)])
        E("dve", lambda e: e.tensor_tensor(out=self.hT[:, :, t * 128:(t + 1) * 128], in0=ntv,
                                            in1=self.gs[:, 2 * n + 1, :].unsqueeze(2).to_broadcast([128, 8, 128]), op=ALU.add),
          reads=[("nt", ni), ("gs", 2 * n + 1)], writes=[("hT", t)])

    def norm_tile(self, t, n, banks=(0, 1, 2, 3)):
        xi = self.norm_a(t)
        self.norm_b(t, n, xi, banks)

    def norm_pipe(self, n, delay=1, lazy_from=10 ** 9):
        pend = []

        def after_tile(t):
            while len(pend) >= delay and pend and pend[0][0] < lazy_from:
                t0, xi0 = pend.pop(0)
                self.norm_b(t0, n, xi0)
            pend.append((t, self.norm_a(t)))

        def ensure(tmax, banks=(4, 5, 6, 7)):
            while pend and pend[0][0] <= tmax:
                t0, xi0 = pend.pop(0)
                self.norm_b(t0, n, xi0, banks)

        def flush():
            ensure(10 ** 9, (0, 1, 2, 3))

        return after_tile, ensure, flush

    def ffn(self, g, fi, lo, extra=None, after_tile=None, next_loads=None, ensure=None):
        E = self.E
        ns = self.cfg.groups[g] + 1
        slots = list(range(lo, ns))
        n = 0 if fi == 0 else 2
        chunks = self.nchunks(lo, ns)
        actT = self.R
        for hh in range(2):
            loaded = {}

            def a5_load(jj, hh=hh):
                j = hh * NJH + jj
                slot, key = self.ring_load(self.d_wgu[fi][j], 8 * 256, ("wgu", fi, j))
                loaded[jj] = (slot[:, 0:2048].rearrange("p (k c) -> p k c", k=8), key)
                if jj == 2:
                    self.gid += 1
                    for part, (a, b_) in enumerate([(0, 4), (4, 8), (8, 11)]):
                        E("pool", lambda e, a=a, b_=b_, hh=hh: e.dma_start(
                            out=self.wdh[:, a:b_, :].rearrange("p a b -> p (a b)"),
                            in_=self.d_wdh[fi][hh][:, a * 1024:b_ * 1024]),
                          writes=[("wdh", part)], dma_sem="wdh", group=self.newgroup())

            def a5_step(jj, c0, c1, hh=hh):
                sv, key = loaded[jj]
                nn = c1 - c0
                if ensure is not None and hh == 0:
                    ensure(self.slots_of(c0, c1)[-1])
                ba, bb = self.rr("psA5", [(0, 1), (2, 3)])
                rd = [key] + [("hT", t) for t in self.slots_of(c0, c1)]
                for half, b in ((0, ba), (1, bb)):
                    for k in range(8):
                        E("pe", lambda e, half=half, b=b, k=k, sv=sv, c0=c0, c1=c1, nn=nn: e.matmul(
                            out=self.bank(b, nn), lhsT=sv[:, k, half * 128:(half + 1) * 128],
                            rhs=self.hT[:, k, c0:c1], start=(k == 0), stop=(k == 7)),
                          reads=rd, writes=[("ps", b)])
                tp, tk = self.tmp512()
                E("act", lambda e, tp=tp, ba=ba, nn=nn: e.activation(out=tp[:, 0:nn], in_=self.bank(ba, nn), func=AF.Silu),
                  reads=[("ps", ba)], writes=[tk])
                E("dve", lambda e, tp=tp, bb=bb, nn=nn, jj=jj, c0=c0, c1=c1: e.tensor_tensor(
                    out=actT[:, jj, c0:c1], in0=tp[:, 0:nn], in1=self.bank(bb, nn), op=ALU.mult),
                  reads=[tk, ("ps", bb)], writes=[("R", jj, t) for t in self.slots_of(c0, c1)])

            nskew = self.cfg.nring if (hh == 0 and ensure is not None and len(chunks) > 1) else 0
            if nskew:
                for jj in range(nskew):
                    a5_load(jj)
                for ci, (c0, c1) in enumerate(chunks):
                    for jj in range(nskew):
                        a5_step(jj, c0, c1)
            for jj in range(nskew, NJH):
                a5_load(jj)
                for (c0, c1) in chunks:
                    a5_step(jj, c0, c1)
            wkeys = [("wdh", p) for p in range(3)]
            if hh == 1 and after_tile is not None:
                self.zero_ss()
            if extra is None:
                if hh == 0:
                    self.prefetch([(self.d_wgu[fi][j_], 8 * 256, ("wgu", fi, j_)) for j_ in range(NJH, NJH + 3)])
                elif next_loads:
                    self.prefetch(next_loads)
            for t in slots:
                b0, b1 = self.rr("psA6", [(4, 5), (6, 7)])
                for jj in range(NJH):
                    for oh, b in ((0, b0), (1, b1)):
                        E("pe", lambda e, jj=jj, oh=oh, b=b, t=t: e.matmul(
                            out=self.bank(b), lhsT=actT[:, jj, t * 128:(t + 1) * 128],
                            rhs=self.wdh[:, jj, oh * 512:(oh + 1) * 512], start=(jj == 0), stop=(jj == NJH - 1)),
                          reads=wkeys + [("R", jj, t)], writes=[("ps", b)])
                for oh, b in ((0, b0), (1, b1)):
                    tp, tk = self.tmp512()
                    E("dve", lambda e, tp=tp, b=b, oh=oh: e.tensor_tensor(
                        out=tp, in0=self.bank(b), in1=self.bc[:, n, oh * 512:(oh + 1) * 512], op=ALU.mult),
                      reads=[("ps", b), ("bc", n)], writes=[tk])
                    E("dve", lambda e, tp=tp, oh=oh, t=t: e.tensor_tensor(
                        out=self.xres[:, t, oh * 512:(oh + 1) * 512], in0=self.xres[:, t, oh * 512:(oh + 1) * 512],
                        in1=tp, op=ALU.add),
                      reads=[tk, ("x", t)], writes=[("x", t)])
                if extra is not None:
                    extra()
                if hh == 1 and after_tile is not None:
                    after_tile(t)
            if hh == 0 and extra is not None:
                extra(upto=15)

    def rope_evac(self, bq, bs, dst, c0, c1):
        E = self.E
        nn = c1 - c0
        t1, k1 = self.tmp512()
        t2, k2 = self.tmp512()
        E("dve", lambda e: e.tensor_tensor(out=t1[:, 0:nn], in0=self.bank(bq, nn), in1=self.cosT[:, c0:c1], op=ALU.mult),
          reads=[("ps", bq), "tab"], writes=[k1])
        E("dve", lambda e: e.tensor_tensor(out=t2[:, 0:nn], in0=self.bank(bs, nn), in1=self.sinT[:, c0:c1], op=ALU.mult),
          reads=[("ps", bs), "tab"], writes=[k2])
        return t1, k1, t2, k2

    def mixer(self, g, after_tile=None, ensure=None):
        E = self.E
        cfg = self.cfg
        nt = cfg.groups[g]
        ns = nt + 1
        lo = 0 if g == 0 else 1
        tile0 = sum(cfg.groups[:g])
        Y = self.R
        MB = 8
        for dst, src in ((self.cosT, self.d_cos), (self.sinT, self.d_sin)):
            E("sp", lambda e, dst=dst, src=src: e.dma_start(out=dst[:, 0:ns * 128], in_=src[:, tile0 * 128:(tile0 + ns) * 128]),
              writes=["tab"], dma_sem="tab", group=("tab", g))
        if g > 0:
            pns = cfg.groups[g - 1] + 1
            E("dve", lambda e: e.tensor_copy(out=self.kT[:, :, 0:128], in_=self.kT[:, :, (pns - 1) * 128:pns * 128]),
              reads=[("kT", gg, pns - 1) for gg in range(4)], writes=[("kT", gg, 0) for gg in range(4)])
            E("dve", lambda e: e.tensor_copy(out=self.vv[:, 0, :], in_=self.vv[:, pns - 1, :]),
              reads=[("v", pns - 1)], writes=[("v", 0)])
        chunks_lo = self.nchunks(lo, ns)
        chunks_1 = self.nchunks(1, ns)
        for j in range(8):
            slot, key = self.ring_load(self.d_wcv[j], 8 * 384, ("wcv", j))
            if j == 2:
                E("pool", lambda e: e.dma_start(out=self.wdh[:, 0:8, :].rearrange("p a b -> p (a b)"), in_=self.d_wout),
                  writes=[("wdh", p) for p in range(3)], dma_sem="wdh", group=self.newgroup(True))
            sv = slot[:, 0:3072].rearrange("p (k c) -> p k c", k=8)
            ci = 0
            cu = self.cu[:, ci, :]
            if g > 0:
                E("dve", lambda e, cu=cu, j=j: e.tensor_copy(out=cu[:, 128:130], in_=self.cucarry[:, j, :]),
                  reads=[("cucarry", j)], writes=[("cu", ci)])
            else:
                E("dve", lambda e, cu=cu: e.memset(cu[:, 0:2], 0.0), writes=[("cu", ci)])
            for ich, (c0, c1) in enumerate(chunks_lo):
                nn = c1 - c0
                if ensure is not None and j == 0:
                    ensure(self.slots_of(c0, c1)[-1], (6, 7))
                b_b, b_c, b_u = self.rr("psCV", [(0, 1, 2), (3, 4, 5)])
                rd = [key] + [("hT", t) for t in self.slots_of(c0, c1)]
                for m, b in ((0, b_b), (1, b_c), (2, b_u)):
                    for k in range(8):
                        E("pe", lambda e, m=m, b=b, k=k, sv=sv, c0=c0, c1=c1, nn=nn: e.matmul(
                            out=self.bank(b, nn), lhsT=sv[:, k, m * 128:(m + 1) * 128], rhs=self.hT[:, k, c0:c1],
                            start=(k == 0), stop=(k == 7)),
                          reads=rd, writes=[("ps", b)])
                ut, uk = self.tmp512()
                E("act", lambda e, ut=ut, b_u=b_u, nn=nn: e.activation(out=ut[:, 0:nn], in_=self.bank(b_u, nn), func=AF.Copy),
                  reads=[("ps", b_u)], writes=[uk])
                bt_, bk_ = self.tmp512()
                E("act", lambda e, bt_=bt_, b_b=b_b, nn=nn: e.activation(out=bt_[:, 0:nn], in_=self.bank(b_b, nn), func=AF.Copy),
                  reads=[("ps", b_b)], writes=[bk_])
                E("dve", lambda e, cu=cu, ut=ut, b_c=b_c, c0=c0, c1=c1, nn=nn: e.tensor_tensor(
                    out=cu[:, 2 + c0:2 + c1], in0=self.bank(b_c, nn), in1=ut[:, 0:nn], op=ALU.mult),
                  reads=[("ps", b_c), uk, ("cu", ci)], writes=[("cu", ci)])
                if g == 0 and ich == 0:
                    E("dve", lambda e, cu=cu: e.tensor_scalar(out=cu[:, 2 + 126:2 + 128], in0=cu[:, 2 + 126:2 + 128],
                                                              scalar1=self.flag[:, 0:1], scalar2=None, op0=ALU.mult),
                      reads=[("cu", ci), "flag"], writes=[("cu", ci)])
                at, ak = self.tmp512()
                E("dve", lambda e, at=at, cu=cu, j=j, c0=c0, c1=c1, nn=nn: e.tensor_scalar(
                    out=at[:, 0:nn], in0=cu[:, 2 + c0:2 + c1], scalar1=self.convw[:, j * 3 + 2:j * 3 + 3], scalar2=None,
                    op0=ALU.mult), reads=[("cu", ci), "convw"], writes=[ak])
                for kk, off in ((1, 1), (0, 0)):
                    E("dve", lambda e, at=at, cu=cu, j=j, c0=c0, c1=c1, nn=nn, kk=kk, off=off: e.scalar_tensor_tensor(
                        out=at[:, 0:nn], in0=cu[:, off + c0:off + c1], scalar=self.convw[:, j * 3 + kk:j * 3 + kk + 1],
                        in1=at[:, 0:nn], op0=ALU.mult, op1=ALU.add), reads=[("cu", ci), ak, "convw"], writes=[ak])
                E("dve", lambda e, at=at, bt_=bt_, j=j, c0=c0, c1=c1, nn=nn: e.tensor_tensor(
                    out=Y[:, j, c0:c1], in0=bt_[:, 0:nn], in1=at[:, 0:nn], op=ALU.mult),
                  reads=[bk_, ak], writes=[("R", j, t) for t in self.slots_of(c0, c1)])
            E("dve", lambda e, cu=cu, j=j: e.tensor_copy(out=self.cucarry[:, j, :], in_=cu[:, ns * 128:ns * 128 + 2]),
              reads=[("cu", ci)], writes=[("cucarry", j)])
        self.gated_proj(self.d_wcp, "wcp", chunks_1, first=True)
        for c in range(8):
            slot, key = self.ring_load(self.d_wq[c], 8 * 256)
            sv = slot[:, 0:2048].rearrange("p (k c) -> p k c", k=8)
            for (c0, c1) in chunks_1:
                nn = c1 - c0
                bq, bs = self.rr("psQ", [(0, 1), (2, 3)])
                rd = [key] + [("hT", t) for t in self.slots_of(c0, c1)]
                for m, b in ((0, bq), (1, bs)):
                    for k in range(8):
                        E("pe", lambda e, m=m, b=b, k=k, sv=sv, c0=c0, c1=c1, nn=nn: e.matmul(
                            out=self.bank(b, nn), lhsT=sv[:, k, m * 128:(m + 1) * 128], rhs=self.hT[:, k, c0:c1],
                            start=(k == 0), stop=(k == 7)), reads=rd, writes=[("ps", b)])
                t1, k1, t2, k2 = self.rope_evac(bq, bs, None, c0, c1)
                E("dve", lambda e, t1=t1, t2=t2, c=c, c0=c0, c1=c1, nn=nn: e.tensor_tensor(
                    out=Y[:, c, c0:c1], in0=t1[:, 0:nn], in1=t2[:, 0:nn], op=ALU.add),
                  reads=[k1, k2], writes=[("R", c, t) for t in self.slots_of(c0, c1)])
        for kg in range(4):
            slot, key = self.ring_load(self.d_wk[kg], 8 * 256)
            sv = slot[:, 0:2048].rearrange("p (k c) -> p k c", k=8)
            for (c0, c1) in chunks_lo:
                nn = c1 - c0
                bq, bs = self.rr("psQ", [(0, 1), (2, 3)])
                rd = [key] + [("hT", t) for t in self.slots_of(c0, c1)]
                for m, b in ((0, bq), (1, bs)):
                    for k in range(8):
                        E("pe", lambda e, m=m, b=b, k=k, sv=sv, c0=c0, c1=c1, nn=nn: e.matmul(
                            out=self.bank(b, nn), lhsT=sv[:, k, m * 128:(m + 1) * 128], rhs=self.hT[:, k, c0:c1],
                            start=(k == 0), stop=(k == 7)), reads=rd, writes=[("ps", b)])
                t1, k1, t2, k2 = self.rope_evac(bq, bs, None, c0, c1)
                E("dve", lambda e, t1=t1, t2=t2, kg=kg, c0=c0, c1=c1, nn=nn: e.tensor_tensor(
                    out=self.kT[:, kg, c0:c1], in0=t1[:, 0:nn], in1=t2[:, 0:nn], op=ALU.add),
                  reads=[k1, k2], writes=[("kT", kg, t) for t in self.slots_of(c0, c1)])
        slot, key = self.ring_load(self.d_wv[0], 8 * 256)
        sv = slot[:, 0:2048].rearrange("p (k c) -> p k c", k=8)
        for t in range(lo, ns):
            b = self.rr("psV", [0, 1, 2, 3])
            for k in range(8):
                E("pe", lambda e, k=k, b=b, t=t, sv=sv: e.matmul(
                    out=self.bank(b, 256), lhsT=self.hT[:, k, t * 128:(t + 1) * 128], rhs=sv[:, k, :],
                    start=(k == 0), stop=(k == 7)), reads=[key, ("hT", t)], writes=[("ps", b)])
            E("act", lambda e, b=b, t=t: e.activation(out=self.vv[:, t, :], in_=self.bank(b, 256), func=AF.Copy),
              reads=[("ps", b)], writes=[("v", t)])
        if cfg.stop == 3 and getattr(cfg, "sub", 9) < 1:
            return
        self.prefetch([(self.d_wap[j_], 8 * 256, ("wap", j_)) for j_ in range(3)])
        self.attention(g, ns)
        if cfg.stop == 3 and getattr(cfg, "sub", 9) < 2:
            return
        self.gated_proj(self.d_wap, "wap", chunks_1, first=False)
        wkeys = [("wdh", p) for p in range(3)]
        self.prefetch([(self.d_wgu[1][j_], 8 * 256, ("wgu", 1, j_)) for j_ in range(3)])
        if after_tile is not None:
            self.zero_ss()
        for t in range(1, ns):
            b0, b1 = self.rr("psA6", [(4, 5), (6, 7)])
            for k in range(8):
                for oh, b in ((0, b0), (1, b1)):
                    E("pe", lambda e, k=k, oh=oh, b=b, t=t: e.matmul(
                        out=self.bank(b), lhsT=self.R[:, MB + k, t * 128:(t + 1) * 128],
                        rhs=self.wdh[:, k, oh * 512:(oh + 1) * 512], start=(k == 0), stop=(k == 7)),
                      reads=wkeys + [("R", MB + k, t)], writes=[("ps", b)])
            for oh, b in ((0, b0), (1, b1)):
                tp, tk = self.tmp512()
                E("dve", lambda e, tp=tp, b=b, oh=oh: e.tensor_tensor(
                    out=tp, in0=self.bank(b), in1=self.bc[:, 1, oh * 512:(oh + 1) * 512], op=ALU.mult),
                  reads=[("ps", b), ("bc", 1)], writes=[tk])
                E("dve", lambda e, tp=tp, oh=oh, t=t: e.tensor_tensor(
                    out=self.xres[:, t, oh * 512:(oh + 1) * 512], in0=self.xres[:, t, oh * 512:(oh + 1) * 512],
                    in1=tp, op=ALU.add), reads=[tk, ("x", t)], writes=[("x", t)])
            if after_tile is not None:
                after_tile(t)

    def gated_proj(self, dsrc, dtag, chunks, first):
        E = self.E
        Y = self.R
        MB = 8
        for j in range(8):
            slot, key = self.ring_load(dsrc[j], 8 * 256, (dtag, j))
            sv = slot[:, 0:2048].rearrange("p (k c) -> p k c", k=8)
            for (c0, c1) in chunks:
                nn = c1 - c0
                by, bz = self.rr("psQ", [(0, 1), (2, 3)])
                sl = self.slots_of(c0, c1)
                for k in range(8):
                    E("pe", lambda e, k=k, by=by, sv=sv, c0=c0, c1=c1, nn=nn: e.matmul(
                        out=self.bank(by, nn), lhsT=sv[:, k, 0:128], rhs=Y[:, k, c0:c1], start=(k == 0), stop=(k == 7)),
                      reads=[key] + [("R", k, t) for t in sl], writes=[("ps", by)])
                for k in range(8):
                    E("pe", lambda e, k=k, bz=bz, sv=sv, c0=c0, c1=c1, nn=nn: e.matmul(
                        out=self.bank(bz, nn), lhsT=sv[:, k, 128:256], rhs=self.hT[:, k, c0:c1], start=(k == 0), stop=(k == 7)),
                      reads=[key] + [("hT", t) for t in sl], writes=[("ps", bz)])
                sg, sk = self.tmp512()
                E("act", lambda e, sg=sg, bz=bz, nn=nn: e.activation(out=sg[:, 0:nn], in_=self.bank(bz, nn), func=AF.Sigmoid),
                  reads=[("ps", bz)], writes=[sk])
                mk = [("R", MB + j, t) for t in sl]
                if first:
                    E("dve", lambda e, sg=sg, by=by, j=j, c0=c0, c1=c1, nn=nn: e.tensor_tensor(
                        out=self.R[:, MB + j, c0:c1], in0=self.bank(by, nn), in1=sg[:, 0:nn], op=ALU.mult),
                      reads=[("ps", by), sk], writes=mk)
                else:
                    E("dve", lambda e, sg=sg, by=by, nn=nn: e.tensor_tensor(
                        out=sg[:, 0:nn], in0=self.bank(by, nn), in1=sg[:, 0:nn], op=ALU.mult),
                      reads=[("ps", by), sk], writes=[sk])
                    E("dve", lambda e, sg=sg, j=j, c0=c0, c1=c1, nn=nn: e.tensor_tensor(
                        out=self.R[:, MB + j, c0:c1], in0=self.R[:, MB + j, c0:c1], in1=sg[:, 0:nn], op=ALU.add),
                      reads=[sk] + mk, writes=mk)

    def attention(self, g, ns):
        E = self.E
        Y = self.R
        yb = (6, 7)
        units = [(t, kg) for t in range(1, ns) for kg in range(4)]
        st = {}

        def tile_state(t):
            if t not in st:
                par = self.rr("astp", [0, 1])
                st[t] = dict(par=par, negm=self.ast[:, par, 0, :], rsum=self.ast[:, par, 1, :],
                             atmp=self.ast[:, par, 2, :], rden=self.ast[:, par, 3, :], akey=("ast", par))
                E("dve", lambda e, r=st[t]["rsum"]: e.memset(r, 0.0), writes=[("rsum", par, k_) for k_ in range(4)])
            return st[t]

        ust = {}

        def S1(u):
            t, kg = units[u]
            tile_state(t)
            b0, b1 = self.rr("psS", [(0, 1), (2, 3)])
            si = self.rr("Sm", [0, 1])
            ust[u] = dict(b0=b0, b1=b1, si=si)
            tc0, tc1 = t * 128, (t + 1) * 128
            for hh in range(4):
                ch, ph = 2 * kg + (hh % 2), (hh // 2) * 64
                b = b0 if hh < 2 else b1
                for kb in range(2):
                    o0 = (hh % 2) * 256 + kb * 128
                    ks = t - 1 + kb
                    E("pe", lambda e, b=b, o0=o0, ch=ch, ph=ph, ks=ks, kg=kg, tc0=tc0, tc1=tc1: e.matmul(
                        out=self.ps[:, b * 512 + o0:b * 512 + o0 + 128], lhsT=Y[ph:ph + 64, ch, tc0:tc1],
                        rhs=self.kT[ph:ph + 64, kg, ks * 128:(ks + 1) * 128], start=True, stop=True),
                      reads=[("R", ch, t), ("kT", kg, ks)], writes=[("ps", b)])

        def S2(u):
            t, kg = units[u]
            ts_ = st[t]
            negm, rsum, par = ts_["negm"], ts_["rsum"], ts_["par"]
            nk, rk = ("negm", par, kg), ("rsum", par, kg)
            b0, b1, si = ust[u]["b0"], ust[u]["b1"], ust[u]["si"]
            mi = 0 if (g == 0 and t == 1) else 1
            for half, b in ((0, b0), (1, b1)):
                E("dve", lambda e, half=half, b=b, si=si, mi=mi: e.scalar_tensor_tensor(
                    out=self.Sm[:, si, 2 * half:2 * half + 2, :],
                    in0=self.bank(b).rearrange("p (a b) -> p a b", a=2), scalar=HD ** -0.5,
                    in1=self.masks[:, mi:mi + 1, :].to_broadcast([128, 2, 256]), op0=ALU.mult, op1=ALU.add),
                  reads=[("ps", b), "masks"], writes=[("Sm", si, half)])
            E("dve", lambda e, si=si, kg=kg, negm=negm: e.tensor_reduce(out=negm[:, 4 * kg:4 * kg + 4], in_=self.Sm[:, si, :, :],
                                                                        axis=AX.X, op=ALU.max),
              reads=[("Sm", si, 0), ("Sm", si, 1)], writes=[nk])
            E("dve", lambda e, kg=kg, negm=negm: e.scalar_tensor_tensor(
                out=negm[:, 4 * kg:4 * kg + 4], in0=negm[:, 4 * kg:4 * kg + 4], scalar=-1.0,
                in1=self.sinkbc[:, 1, 4 * kg:4 * kg + 4], op0=ALU.mult, op1=ALU.min),
              reads=[nk, "sink1"], writes=[nk])
            for hh in range(4):
                h = 4 * kg + hh
                E("act", lambda e, si=si, hh=hh, h=h, negm=negm, rsum=rsum: e.activation(
                    out=self.P[:, si, hh, :], in_=self.Sm[:, si, hh, :], func=AF.Exp, bias=negm[:, h:h + 1],
                    accum_out=rsum[:, h:h + 1]),
                  reads=[("Sm", si, hh // 2), nk, rk], writes=[("P", si, hh), rk])

        def S3a(u):
            si = ust[u]["si"]
            bt = self.rr("psPT", [4, 5])
            ptv = self.bank_bf(bt)
            for hh in range(4):
                for kb in range(2):
                    o = (hh * 2 + kb) * 128
                    E("pe", lambda e, si=si, hh=hh, kb=kb, o=o, ptv=ptv: e.transpose(
                        out=ptv[:, o:o + 128], in_=self.P[:, si, hh, kb * 128:(kb + 1) * 128], identity=self.idb[:]),
                      reads=[("P", si, hh), "idb"], writes=[("ps", bt)])
            if u % 2 == 0:
                E("dve", lambda e, si=si, ptv=ptv: e.tensor_copy(out=self.pT[:, si, :], in_=ptv),
                  reads=[("ps", bt)], writes=[("pT", si)])
            else:
                E("act", lambda e, si=si, ptv=ptv: e.activation(out=self.pT[:, si, :], in_=ptv, func=AF.Copy),
                  reads=[("ps", bt)], writes=[("pT", si)])

        def S3b(u):
            t, kg = units[u]
            si = ust[u]["si"]
            for hh in range(4):
                h = 4 * kg + hh
                b = yb[h // 8]
                for kb in range(2):
                    o = (hh * 2 + kb) * 128
                    ks = t - 1 + kb
                    E("pe", lambda e, si=si, o=o, ks=ks, kg=kg, h=h, kb=kb, b=b: e.matmul(
                        out=self.ps[:, b * 512 + (h % 8) * 64:b * 512 + (h % 8 + 1) * 64], lhsT=self.pT[:, si, o:o + 128],
                        rhs=self.vv[:, ks, kg * 64:(kg + 1) * 64], start=(kb == 0), stop=(kb == 1)),
                      reads=[("pT", si), ("v", ks)], writes=[("ps", b)])
            if kg == 3:
                tile_end_a1(t)

        def tile_end_a1(t):
            ts_ = st[t]
            negm, atmp, par = ts_["negm"], ts_["atmp"], ts_["par"]
            nks = [("negm", par, k_) for k_ in range(4)]
            tk_ = ("atmp", par)
            E("dve", lambda e: e.tensor_tensor(out=atmp, in0=negm, in1=self.sinkbc[:, 0, :], op=ALU.add),
              reads=nks + ["sink0"], writes=[tk_])
            E("act", lambda e: e.activation(out=atmp, in_=atmp, func=AF.Exp), reads=[tk_], writes=[tk_])

        def tile_end_a2(t):
            ts_ = st[t]
            negm, rsum, atmp, rden, par = ts_["negm"], ts_["rsum"], ts_["atmp"], ts_["rden"], ts_["par"]
            nks = [("negm", par, k_) for k_ in range(4)]
            rks = [("rsum", par, k_) for k_ in range(4)]
            tk_, dk_ = ("atmp", par), ("rden", par)
            E("dve", lambda e: e.tensor_tensor(out=atmp, in0=atmp, in1=rsum, op=ALU.add), reads=[tk_] + rks, writes=[tk_])
            E("dve", lambda e: e.reciprocal(out=rden, in_=atmp), reads=[tk_], writes=[dk_])
            for half in range(2):
                E("dve", lambda e, half=half: e.tensor_tensor(
                    out=self.ybf[:, half * 512:(half + 1) * 512].rearrange("p (a b) -> p a b", a=8),
                    in0=self.bank(yb[half]).rearrange("p (a b) -> p a b", a=8),
                    in1=rden[:, 8 * half:8 * half + 8].unsqueeze(2).to_broadcast([128, 8, 64]), op=ALU.mult),
                  reads=[("ps", yb[half]), dk_], writes=[("ybf", half)])

        def tile_end_b(t):
            tc0, tc1 = t * 128, (t + 1) * 128
            bt = self.rr("psPT", [4, 5])
            ytv = self.bank_bf(bt)
            for k in range(8):
                E("pe", lambda e, k=k, ytv=ytv: e.transpose(out=ytv[:, k * 128:(k + 1) * 128],
                                                            in_=self.ybf[:, k * 128:(k + 1) * 128], identity=self.idb[:]),
                  reads=[("ybf", k // 4), "idb"], writes=[("ps", bt)])
            E("act", lambda e, ytv=ytv: e.activation(out=Y[:, 0:8, tc0:tc1], in_=ytv.rearrange("p (a b) -> p a b", a=8),
                                                     func=AF.Copy),
              reads=[("ps", bt)], writes=[("R", k, t) for k in range(8)])

        n = len(units)
        for s_ in range(n + 6):
            if s_ < n:
                S1(s_)
            if 0 <= s_ - 1 < n:
                S2(s_ - 1)
            if 0 <= s_ - 2 < n:
                S3a(s_ - 2)
            if 0 <= s_ - 4 < n and units[s_ - 4][1] == 3:
                tile_end_a2(units[s_ - 4][0])
            if 0 <= s_ - 3 < n:
                S3b(s_ - 3)
            if 0 <= s_ - 5 < n and units[s_ - 5][1] == 3:
                tile_end_b(units[s_ - 5][0])

    def final_stats(self, t):
        oi = self.rr("nt", [0, 1])
        fc = self.cfg.nsmax + t
        E = self.E
        E("act", lambda e: e.activation(out=self.nt[:, oi, :].bitcast(BF16)[:, 0:D], in_=self.xres[:, t, :], func=AF.Square,
                                        accum_out=self.stat[:, 0, fc:fc + 1]),
          reads=[("x", t), ("ss", fc)], writes=[("ss", fc), ("nt", oi)])
        E("act", lambda e: e.activation(out=self.stat[:, 1, fc:fc + 1], in_=self.stat[:, 0, fc:fc + 1], func=AF.Sqrt,
                                        bias=self.epsc[:, 0:1], scale=1.0 / D),
          reads=[("ss", fc), "epsc"], writes=[("rstd", fc)])
        return oi

    def final_finish(self, g, t, oi):
        E = self.E
        tile0 = sum(self.cfg.groups[:g])
        fc = self.cfg.nsmax + t
        E("dve", lambda e: e.reciprocal(out=self.stat[:, 1, fc:fc + 1], in_=self.stat[:, 1, fc:fc + 1]),
          reads=[("rstd", fc)], writes=[("rstd", fc)])
        E("dve", lambda e: e.scalar_tensor_tensor(
            out=self.nt[:, oi, :], in0=self.xres[:, t, :], scalar=self.stat[:, 1, fc:fc + 1], in1=self.bc[:, 3, :],
            op0=ALU.mult, op1=ALU.mult), reads=[("x", t), ("rstd", fc), ("bc", 3)], writes=[("nt", oi)])
        E("sp", lambda e: e.dma_start(out=self.d_out[tile0 + t - 1], in_=self.nt[:, oi, :]),
          reads=[("nt", oi)], writes=[("out", tile0 + t - 1)], dma_sem="st%d" % oi)

    def load_x_tile(self, g, t):
        tile0 = sum(self.cfg.groups[:g])
        self.E("sp", lambda e: e.dma_start(out=self.xres[:, t, :], in_=self.d_x[tile0 + t]),
               writes=[("x", t)], dma_sem="x%d" % t)

    def build(self):
        nc, cfg = self.nc, self.cfg
        self.declare_dram()
        with ExitStack() as es:
            es.enter_context(nc.allow_low_precision("bf16 matmul operands, fp32 accumulation"))
            self.alloc(es)
            self.setup()
            ng = len(cfg.groups)
            for t in range(0, cfg.groups[0] + 1):
                self.load_x_tile(0, t)
            for l in range(8):
                self.ada_load(l)
            self.ada_finish(0)
            self.zero_ss()
            for t in range(0, cfg.groups[0] + 1):
                self.norm_tile(t, 0, banks=(4, 5, 6))
            pending = list(range(8, N_ADA_LOADS))

            def extra(upto=None):
                while pending:
                    l = pending.pop(0)
                    self.ada_load(l)
                    if l == 15:
                        self.ada_finish(1)
                    if l == 23:
                        self.ada_finish(2)
                    if upto is None or l >= upto:
                        break

            ens_prev = None
            for g in range(ng):
                lo = 0 if g == 0 else 1
                ns = cfg.groups[g] + 1
                at, ens1, fl = self.norm_pipe(1, lazy_from=(5 if g > 0 else 4))
                self.ffn(g, 0, lo, extra=extra if g == 0 else None, after_tile=at, ensure=ens_prev,
                         next_loads=[(self.d_wcv[j_], 8 * 384, ("wcv", j_)) for j_ in range(3)])
                while pending:
                    extra()
                at, ens2, fl2 = self.norm_pipe(2, lazy_from=5)
                self.mixer(g, after_tile=at, ensure=ens1)
                fl()
                nns = cfg.groups[g + 1] + 1 if g + 1 < ng else 0
                at0, ens0, fl0 = self.norm_pipe(0, delay=1, lazy_from=5)

                xl_pend = []

                def after_ffn2(t, g=g, nns=nns, at0=at0, xl_pend=xl_pend):
                    oi = self.final_stats(t)
                    if xl_pend:
                        at0(xl_pend.pop(0))
                    self.final_finish(g, t, oi)
                    if t < nns:
                        self.load_x_tile(g + 1, t)
                        xl_pend.append(t)

                self.ffn(g, 1, 1, after_tile=after_ffn2, ensure=ens2,
                         next_loads=[(self.d_wgu[0][j_], 8 * 256, ("wgu", 0, j_)) for j_ in range(3)] if g + 1 < ng else None)
                fl2()
                while xl_pend:
                    at0(xl_pend.pop(0))
                for t in range(ns, nns):
                    self.load_x_tile(g + 1, t)
                    at0(t)
                ens_prev = ens0
            self.E("sp", None, reads=[("out", i) for i in range(cfg.ntc)])
            streams = self.tr.finalize()
            sems = {}
            for e_ in Tracker.ENGS:
                sems[("E", e_)] = es.enter_context(nc.semaphore("s_" + e_))
            for k in self.tr.dma_sems:
                sems[("D", k)] = es.enter_context(nc.semaphore("d_" + k))
            block = es.enter_context(nc.Block())

            def replay(eng, name):
                mysem = sems[("E", name)]
                for item in streams[name]:
                    if item[0] == "wait":
                        eng.wait_ge(sems[item[1]], item[2])
                    else:
                        op = item[1]
                        ins = op.fn(eng)
                        if op.dma_sem is not None:
                            ins.then_inc(sems[("D", op.dma_sem)], 16)
                        elif op.needed:
                            ins.then_inc(mysem, 1)

            @block.tensor
            def _(e):
                replay(e, "pe")

            @block.vector
            def _(e):
                replay(e, "dve")

            @block.scalar
            def _(e):
                replay(e, "act")

            @block.gpsimd
            def _(e):
                replay(e, "pool")

            @block.sync
            def _(e):
                replay(e, "sp")
        return nc


def _pack(W, cols):
    K = W.shape[0]
    Wc = W[:, cols]
    return np.ascontiguousarray(Wc.reshape(K // 128, 128, len(cols)).transpose(1, 0, 2)).reshape(128, -1)


def _col(v):
    return np.ascontiguousarray(v.reshape(-1, 128).T)


def prepare_weights(inp):
    f = lambda a: np.asarray(a, dtype=np.float32)
    ar = np.arange
    w_ada = f(inp["w_ada"])[0]
    w_in = f(inp["w_in"])[0]
    out = {}
    out["wada"] = np.stack([_pack(w_ada, ar(l * 384, (l + 1) * 384)) for l in range(N_ADA_LOADS)])
    for fi, (gu, dn) in enumerate((("w1_gu", "w1_down"), ("w2_gu", "w2_down"))):
        wgu = f(inp[gu])[0]
        wd = f(inp[dn])[0]
        out["wgu%d" % fi] = np.stack([_pack(wgu, np.concatenate([ar(j * 128, (j + 1) * 128), DFF + ar(j * 128, (j + 1) * 128)]))
                                      for j in range(NJ)])
        out["wdh%d" % fi] = np.stack([_pack(wd[hh * NJH * 128:(hh + 1) * NJH * 128], ar(D)) for hh in range(2)])
    out["wcv"] = np.stack([_pack(w_in, np.concatenate([O_B + ar(j * 128, (j + 1) * 128), O_C + ar(j * 128, (j + 1) * 128),
                                                        O_U + ar(j * 128, (j + 1) * 128)])) for j in range(8)])

    def swp(base):
        return np.concatenate([base + ar(32, 64), base + ar(0, 32)])

    def qheads(c):
        g_, e_ = c // 2, c % 2
        return 4 * g_ + e_, 4 * g_ + 2 + e_

    out["wq"] = np.stack([_pack(w_in, np.concatenate([O_Q + qheads(c)[0] * 64 + ar(64), O_Q + qheads(c)[1] * 64 + ar(64),
                                                       swp(O_Q + qheads(c)[0] * 64), swp(O_Q + qheads(c)[1] * 64)]))
                          for c in range(8)])
    out["wk"] = np.stack([_pack(w_in, np.concatenate([O_K + g * 64 + ar(64), O_K + g * 64 + ar(64),
                                                       swp(O_K + g * 64), swp(O_K + g * 64)])) for g in range(4)])
    out["wv"] = _pack(w_in, O_V + ar(256))[None]
    wcp = f(inp["w_conv_proj"])[0]
    wap = f(inp["w_attn_proj"])[0]
    out["wcp"] = np.stack([np.concatenate([_pack(wcp, ar(j * 128, (j + 1) * 128)).reshape(128, 8, 128),
                                            _pack(w_in, O_ZC + ar(j * 128, (j + 1) * 128)).reshape(128, 8, 128)],
                                           axis=2).reshape(128, -1) for j in range(8)])
    out["wap"] = np.stack([np.concatenate([_pack(wap, ar(j * 128, (j + 1) * 128)).reshape(128, 8, 128),
                                            _pack(w_in, O_ZA + ar(j * 128, (j + 1) * 128)).reshape(128, 8, 128)],
                                           axis=2).reshape(128, -1) for j in range(8)])
    out["wout"] = _pack(f(inp["w_out"])[0], ar(D))
    out["gcol"] = np.concatenate([_col(f(inp[n])[0]) for n in ("g_ffn1", "g_mix", "g_ffn2")], axis=1)
    out["badacol"] = _col(f(inp["b_ada"])[0])
    cw = f(inp["conv_w"])[0]
    out["convw"] = np.ascontiguousarray(cw.reshape(3, 8, 128).transpose(2, 1, 0)).reshape(128, 24)
    out["gfin"] = f(inp["g_final"])
    out["sinks"] = f(inp["sinks"])[0]
    out["ident"] = np.eye(128, dtype=np.float32)
    return out


def rope_tables_T(pos):
    inv = (1.0 / (ROPE_THETA ** (np.arange(0, HD, 2, dtype=np.float32) / np.float32(HD)))).astype(np.float32)
    ang = (pos.astype(np.float32)[None, :] * inv[:, None]).astype(np.float32)
    cos = np.cos(ang).astype(np.float32)
    sin = np.sin(ang).astype(np.float32)
    cosT = np.tile(cos, (4, 1))
    sgn = np.where((np.arange(128) % 64) < 32, -1.0, 1.0).astype(np.float32)
    sinT = np.tile(sin, (4, 1)) * sgn[:, None]
    return np.ascontiguousarray(cosT), np.ascontiguousarray(sinT)


def band_masks(first_half):
    qi = np.arange(128)[:, None]
    kj = np.arange(256)[None, :]
    diff = 128 + qi - kj
    valid = (diff >= 0) & (diff < 128)
    reg = np.where(valid, 0.0, NEG).astype(np.float32)
    m0 = reg.copy()
    if first_half:
        m0[:, 0:128] = NEG
    return np.ascontiguousarray(np.concatenate([m0, reg], axis=1))


def core_inputs(cfg, wts, x, c, b, s):
    n = cfg.ntc * 128
    xin = np.zeros((cfg.ntc + 1, 128, D), dtype=np.float32)
    xin[1:] = x[b, s * n:(s + 1) * n].reshape(cfg.ntc, 128, D)
    if s > 0:
        xin[0] = x[b, s * n - 128:s * n]
    pos = np.arange(s * n - 128, (s + 1) * n)
    cosT, sinT = rope_tables_T(pos)
    m = dict(wts)
    m["xin"] = xin
    m["ccol"] = _col(c[b])
    m["masks"] = band_masks(s == 0)
    m["flag"] = np.full((128, 1), 0.0 if s == 0 else 1.0, dtype=np.float32)
    m["cost"] = cosT
    m["sint"] = sinT
    return m


_NC_CACHE = {}


def run(cfg, inputs, nsplit, trace=False):
    x = np.asarray(inputs["x"], dtype=np.float32)
    c = np.asarray(inputs["c"], dtype=np.float32)
    B, S, _ = x.shape
    assert S == nsplit * cfg.ntc * 128
    wts = prepare_weights(inputs)
    in_maps = []
    for b in range(B):
        for s in range(nsplit):
            in_maps.append(core_inputs(cfg, wts, x, c, b, s))
    key = (cfg.ntc, tuple(cfg.groups), cfg.nring)
    if key not in _NC_CACHE:
        _NC_CACHE[key] = Builder(cfg).build()
    nc = _NC_CACHE[key]
    res = run_bass_kernel_spmd(nc, in_maps, core_ids=list(range(len(in_maps))), trace=trace)
    out = np.empty((B, S, D), dtype=np.float32)
    n = cfg.ntc * 128
    i = 0
    for b in range(B):
        for s in range(nsplit):
            out[b, s * n:(s + 1) * n] = np.asarray(res.results[i]["out"]).reshape(n, D)
            i += 1
    return out, res


def kernel(**inputs):
    cfg = Cfg(ntc=32, groups=(7, 7, 6, 6, 6), nring=3, ntmp=4)
    out, _ = run(cfg, inputs, nsplit=2)
    return out
```

```python
from contextlib import ExitStack

import numpy as np
import concourse.bass as bass
import concourse.mybir as mybir
from concourse.bass_utils import run_bass_kernel_spmd

F32 = mybir.dt.float32
BF16 = mybir.dt.bfloat16
ALU = mybir.AluOpType
AF = mybir.ActivationFunctionType
AX = mybir.AxisListType

D = 1024
DFF = 2816
NJ = 22
NJH = 11
HD = 64
NH = 16
NKV = 4
EPS = 1e-6
NEG = -1e30
ROPE_THETA = 10000.0
RING_ELEMS = 3072
N_ADA_LOADS = 24

O_B, O_C, O_U, O_Q, O_K, O_V, O_ZC, O_ZA = 0, 1024, 2048, 3072, 4096, 4352, 4608, 5632


class Cfg:
    def __init__(self, ntc=32, groups=(8, 8, 8, 8), nring=4, ntmp=8, stop=99):
        assert sum(groups) == ntc
        self.stop = stop
        self.ntc = ntc
        self.groups = list(groups)
        self.nsmax = max(groups) + 1
        self.ncmax = self.nsmax * 128
        self.nring = nring
        self.ntmp = ntmp


class _Op:
    __slots__ = ("eng", "fn", "deps", "dma_sem", "sigval", "needed", "group")

    def __init__(self, eng, fn, deps, dma_sem, group=None):
        self.group = group
        self.eng = eng
        self.fn = fn
        self.deps = deps
        self.dma_sem = dma_sem
        self.sigval = 0
        self.needed = False


class Tracker:
    ENGS = ("pe", "act", "dve", "pool", "sp")

    def __init__(self):
        self.ops = []
        self.last_w = {}
        self.rd_eng = {}
        self.rd_dma = {}

    def emit(self, eng, fn, reads=(), writes=(), dma_sem=None, group=None):
        i = len(self.ops)
        deps = set()
        for r in reads:
            w = self.last_w.get(r)
            if w is not None:
                deps.add(w)
        for wr in writes:
            w = self.last_w.get(wr)
            if w is not None:
                deps.add(w)
            for d in self.rd_eng.get(wr, {}).values():
                deps.add(d)
            for d in self.rd_dma.get(wr, ()):
                deps.add(d)
        if eng == "pe":
            deps = {d for d in deps if not (self.ops[d].eng == "pe" and self.ops[d].dma_sem is None)}
        if group is not None:
            deps = {d for d in deps if not (self.ops[d].group == group and self.ops[d].dma_sem == dma_sem)}
        self.ops.append(_Op(eng, fn, deps, dma_sem, group))
        for r in reads:
            if dma_sem is not None:
                self.rd_dma.setdefault(r, []).append(i)
            else:
                self.rd_eng.setdefault(r, {})[eng] = i
        for wr in writes:
            self.last_w[wr] = i
            self.rd_eng[wr] = {}
            self.rd_dma[wr] = []
        return i

    def finalize(self):
        ops = self.ops
        for op in ops:
            for d in op.deps:
                ops[d].needed = True
        cnt = {e: 0 for e in self.ENGS}
        dcnt = {}
        for op in ops:
            if op.dma_sem is not None:
                dcnt[op.dma_sem] = dcnt.get(op.dma_sem, 0) + 16
                op.sigval = dcnt[op.dma_sem]
            elif op.needed:
                cnt[op.eng] += 1
                op.sigval = cnt[op.eng]
        gmax = {}
        for op in ops:
            if op.group is not None:
                gk = (op.dma_sem, op.group)
                gmax[gk] = max(gmax.get(gk, 0), op.sigval)
        for op in ops:
            if op.group is not None:
                op.sigval = gmax[(op.dma_sem, op.group)]
        streams = {e: [] for e in self.ENGS}
        known = {e: {} for e in self.ENGS}
        nwait = 0
        for op in ops:
            waits = {}
            for d in op.deps:
                dop = ops[d]
                key = ("D", dop.dma_sem) if dop.dma_sem is not None else ("E", dop.eng)
                if waits.get(key, 0) < dop.sigval:
                    waits[key] = dop.sigval
            kn = known[op.eng]
            for key, val in waits.items():
                if kn.get(key, 0) < val:
                    streams[op.eng].append(("wait", key, val))
                    kn[key] = val
                    nwait += 1
            if op.fn is not None:
                streams[op.eng].append(("op", op))
        self.dma_sems = sorted(dcnt.keys())
        self.nwait = nwait
        return streams


class Builder:
    def __init__(self, cfg):
        self.cfg = cfg
        self.nc = bass.Bass("TRN2", target_bir_lowering=False)
        self.tr = Tracker()
        self.ring_i = 0
        self.ring_loads = 0
        self.tmp_i = 0
        self.ps_rr = {}
        self.gid = 0
        self.prefetched = []

    def declare_dram(self):
        nc, cfg = self.nc, self.cfg
        di = lambda n, s: nc.dram_tensor(n, list(s), F32, kind="ExternalInput").ap()
        self.d_x = di("xin", [cfg.ntc + 1, 128, D])
        self.d_ccol = di("ccol", [128, 8])
        self.d_gcol = di("gcol", [128, 24])
        self.d_bada = di("badacol", [128, 72])
        self.d_convw = di("convw", [128, 24])
        self.d_gfin = di("gfin", [D])
        self.d_sinks = di("sinks", [NH])
        self.d_ident = di("ident", [128, 128])
        self.d_masks = di("masks", [128, 512])
        self.d_flag = di("flag", [128, 1])
        self.d_cos = di("cost", [128, (cfg.ntc + 1) * 128])
        self.d_sin = di("sint", [128, (cfg.ntc + 1) * 128])
        self.d_wada = di("wada", [N_ADA_LOADS, 128, 8 * 384])
        self.d_wgu = [di("wgu%d" % f, [NJ, 128, 8 * 256]) for f in range(2)]
        self.d_wdh = [di("wdh%d" % f, [2, 128, NJH * 1024]) for f in range(2)]
        self.d_wcv = di("wcv", [8, 128, 8 * 384])
        self.d_wq = di("wq", [8, 128, 8 * 256])
        self.d_wk = di("wk", [4, 128, 8 * 256])
        self.d_wv = di("wv", [1, 128, 8 * 256])
        self.d_wcp = di("wcp", [8, 128, 8 * 256])
        self.d_wap = di("wap", [8, 128, 8 * 256])
        self.d_wout = di("wout", [128, 8 * 1024])
        self.d_out = nc.dram_tensor("out", [cfg.ntc, 128, D], F32, kind="ExternalOutput").ap()

    def alloc(self, es):
        nc, cfg = self.nc, self.cfg
        NS, NC = cfg.nsmax, cfg.ncmax
        sb = lambda n, s, dt: es.enter_context(nc.sbuf_tensor(n, list(s), dt))
        self.xres = sb("xres", [128, NS, D], F32)
        self.hT = sb("hT", [128, 8, NC], BF16)
        self.R = sb("R", [128, 16, NC], BF16)
        self.wdh = sb("wdhb", [128, NJH, 1024], BF16)
        self.kT = sb("kT", [128, 4, NC], BF16)
        self.vv = sb("vv", [128, NS, 256], BF16)
        self.ring = sb("ring", [128, cfg.nring, RING_ELEMS], BF16)
        self.bc = sb("bc", [128, 4, D], F32)
        self.cosT = sb("cosT", [128, NC], F32)
        self.sinT = sb("sinT", [128, NC], F32)
        self.tmp = sb("tmp", [128, cfg.ntmp, 512], F32)
        self.xn = sb("xn", [128, 4, D], BF16)
        self.nt = sb("nt", [128, 2, D], F32)
        self.cu = sb("cu", [128, 1, NC + 2], F32)
        self.cucarry = sb("cucarry", [128, 8, 2], F32)
        self.Sm = sb("Sm", [128, 2, 4, 256], F32)
        self.P = sb("P", [128, 2, 4, 256], BF16)
        self.pT = sb("pT", [128, 2, 1024], BF16)
        self.ybf = sb("ybf", [128, D], BF16)
        self.ast = sb("ast", [128, 2, 4, 16], F32)
        self.stat = sb("stat", [128, 2, 2 * NS], F32)
        self.epsc = sb("epsc", [128, 1], F32)
        self.idf = sb("idf", [128, 128], F32)
        self.idb = sb("idb", [128, 128], BF16)
        self.ones = sb("ones", [128, 128], F32)
        self.ccol = sb("ccols", [128, 8], F32)
        self.cbf = sb("cbf", [128, 8], BF16)
        self.gcol = sb("gcols", [128, 24], F32)
        self.bada = sb("badas", [128, 72], F32)
        self.modcol = sb("modcol", [128, 72], F32)
        self.gs = sb("gs", [128, 6, 8], F32)
        self.convw = sb("convws", [128, 24], F32)
        self.sinkbc = sb("sinkbc", [128, 2, NH], F32)
        self.masks = sb("maskss", [128, 2, 256], F32)
        self.flag = sb("flags", [128, 1], F32)
        self.ps = es.enter_context(nc.psum_tensor("ps", [128, 4096], F32))

    def bank(self, b, n=512):
        return self.ps[:, b * 512:b * 512 + n]

    def bank_bf(self, b):
        return self.ps[:, b * 512:(b + 1) * 512].bitcast(BF16)

    def rr(self, name, options):
        i = self.ps_rr.get(name, 0)
        self.ps_rr[name] = i + 1
        return options[i % len(options)]

    def tmp512(self):
        i = self.tmp_i % self.cfg.ntmp
        self.tmp_i += 1
        return self.tmp[:, i, :], ("tmp", i)

    def E(self, eng, fn, reads=(), writes=(), dma_sem=None, group=None):
        return self.tr.emit(eng, fn, reads, writes, dma_sem, group)

    def newgroup(self, bump=False):
        if bump:
            self.gid += 1
        return self.gid

    def _ring_issue(self, src, nelem):
        i = self.ring_i % self.cfg.nring
        self.ring_i += 1
        dst = self.ring[:, i, 0:nelem]
        self.E("pool", lambda e, dst=dst, src=src: e.dma_start(out=dst, in_=src),
               reads=(), writes=[("ring", i)], dma_sem="ring%d" % i)
        return self.ring[:, i, :], ("ring", i)

    def ring_load(self, src, nelem, tag=None):
        if self.prefetched:
            ptag, slot, key = self.prefetched.pop(0)
            assert ptag == tag and tag is not None, (ptag, tag)
            return slot, key
        return self._ring_issue(src, nelem)

    def prefetch(self, loads):
        assert not self.prefetched
        for src, nelem, tag in loads[:self.cfg.nring]:
            slot, key = self._ring_issue(src, nelem)
            self.prefetched.append((tag, slot, key))

    def nchunks(self, lo, ns):
        out = []
        c = lo * 128
        end = ns * 128
        while c < end:
            n = min(512, end - c)
            out.append((c, c + n))
            c += n
        return out

    @staticmethod
    def slots_of(c0, c1):
        return list(range(c0 // 128, (c1 + 127) // 128))

    def setup(self):
        E = self.E
        ld = lambda dst, src, w: E("sp", lambda e: e.dma_start(out=dst, in_=src), writes=[w], dma_sem="const", group=0)
        ld(self.ccol[:], self.d_ccol, "ccol")
        ld(self.bada[:], self.d_bada, "bada")
        ld(self.gcol[:], self.d_gcol, "gcol")
        ld(self.idf[:], self.d_ident, "idf")
        ld(self.convw[:], self.d_convw, "convw")
        ld(self.masks[:].rearrange("p a b -> p (a b)"), self.d_masks, "masks")
        ld(self.flag[:], self.d_flag, "flag")
        ld(self.sinkbc[:, 0, :], self.d_sinks.partition_broadcast(128), "sink0")
        ld(self.bc[:, 3, :], self.d_gfin.partition_broadcast(128), ("bc", 3))
        E("dve", lambda e: e.tensor_copy(out=self.idb[:], in_=self.idf[:]), reads=["idf"], writes=["idb"])
        E("dve", lambda e: e.memset(self.ones[:], 1.0), writes=["ones"])
        E("dve", lambda e: e.memset(self.epsc[:], EPS), writes=["epsc"])
        E("dve", lambda e: e.memset(self.cucarry[:], 0.0), writes=["cucarry"])
        E("dve", lambda e: e.tensor_scalar(out=self.sinkbc[:, 1, :], in0=self.sinkbc[:, 0, :], scalar1=-1.0,
                                           scalar2=None, op0=ALU.mult), reads=["sink0"], writes=["sink1"])
        E("act", lambda e: e.activation(out=self.ccol[:], in_=self.ccol[:], func=AF.Silu),
          reads=["ccol"], writes=["ccol"])
        E("dve", lambda e: e.tensor_copy(out=self.cbf[:], in_=self.ccol[:]), reads=["ccol"], writes=["cbf"])

    def ada_load(self, l):
        E = self.E
        slot, key = self.ring_load(self.d_wada[l], 8 * 384)
        sv = slot[:, 0:8 * 384].rearrange("p (k c) -> p k c", k=8)
        for mi in range(3):
            mc = 3 * l + mi
            for k in range(8):
                E("pe", lambda e, mc=mc, k=k, mi=mi, sv=sv: e.matmul(
                    out=self.ps[:, 7 * 512 + mc:7 * 512 + mc + 1], lhsT=sv[:, k, mi * 128:(mi + 1) * 128],
                    rhs=self.cbf[:, k:k + 1], start=(k == 0), stop=(k == 7)),
                  reads=[key, "cbf"], writes=[("ps", 7)])
        E("dve", lambda e: e.tensor_tensor(out=self.modcol[:, 3 * l:3 * l + 3], in0=self.ps[:, 7 * 512 + 3 * l:7 * 512 + 3 * l + 3],
                                           in1=self.bada[:, 3 * l:3 * l + 3], op=ALU.add),
          reads=[("ps", 7), "bada"], writes=[("modcol", l)])

    def ada_finish(self, n):
        E = self.E
        c0 = 24 * n
        mrd = [("modcol", l) for l in range(8 * n, 8 * n + 8)]
        E("dve", lambda e: e.tensor_copy(out=self.gs[:, 2 * n + 1, :], in_=self.modcol[:, c0:c0 + 8]),
          reads=mrd, writes=[("gs", 2 * n + 1)])
        E("dve", lambda e: e.scalar_tensor_tensor(out=self.gs[:, 2 * n, :], in0=self.modcol[:, c0 + 8:c0 + 16], scalar=1.0,
                                                  in1=self.gcol[:, 8 * n:8 * n + 8], op0=ALU.add, op1=ALU.mult),
          reads=mrd + ["gcol"], writes=[("gs", 2 * n)])
        gsc = 1.0 if n == 1 else 0.5
        for k in range(8):
            E("dve", lambda e, k=k: e.tensor_scalar(out=self.nt[:, 0, k * 128:(k + 1) * 128], in0=self.idf[:],
                                                    scalar1=self.modcol[:, c0 + 16 + k:c0 + 17 + k], scalar2=gsc,
                                                    op0=ALU.mult, op1=ALU.mult),
              reads=mrd + ["idf"], writes=[("nt", 0)])
        for half in range(2):
            b = 5 + half
            E("pe", lambda e, half=half, b=b: e.matmul(
                out=self.bank(b), lhsT=self.ones[:],
                rhs=self.nt[:, 0, half * 512:(half + 1) * 512], start=True, stop=True),
              reads=["ones", ("nt", 0)], writes=[("ps", b)])
            E("act", lambda e, half=half, b=b: e.activation(out=self.bc[:, n, half * 512:(half + 1) * 512],
                                                            in_=self.bank(b), func=AF.Copy),
              reads=[("ps", b)], writes=[("bc", n)])

    def zero_ss(self):
        nsc = 2 * self.cfg.nsmax
        self.E("dve", lambda e: e.memset(self.stat[:, 0, :], 0.0), writes=[("ss", c) for c in range(nsc)])

    def rstd_tile(self, t, junk, jkey, col=None):
        E = self.E
        col = t if col is None else col
        E("act", lambda e: e.activation(out=junk, in_=self.xres[:, t, :], func=AF.Square,
                                        accum_out=self.stat[:, 0, col:col + 1]),
          reads=[("x", t), ("ss", col)], writes=[("ss", col), jkey])
        E("act", lambda e: e.activation(out=self.stat[:, 1, col:col + 1], in_=self.stat[:, 0, col:col + 1], func=AF.Sqrt,
                                        bias=self.epsc[:, 0:1], scale=1.0 / D),
          reads=[("ss", col), "epsc"], writes=[("rstd", col)])
        E("dve", lambda e: e.reciprocal(out=self.stat[:, 1, col:col + 1], in_=self.stat[:, 1, col:col + 1]),
          reads=[("rstd", col)], writes=[("rstd", col)])

    def norm_a(self, t):
        E = self.E
        xi = self.rr("xn", [0, 1, 2, 3])
        self.rstd_tile(t, self.xn[:, xi, :], ("xn", xi))
        E("act", lambda e: e.activation(out=self.xn[:, xi, :], in_=self.xres[:, t, :], func=AF.Copy,
                                        scale=self.stat[:, 1, t:t + 1]),
          reads=[("x", t), ("rstd", t)], writes=[("xn", xi)])
        return xi

    def norm_b(self, t, n, xi, banks=(0, 1, 2, 3)):
        E = self.E
        bb = self.rr("psT", list(banks))
        pv = self.bank_bf(bb)
        for k in range(8):
            E("pe", lambda e, k=k: e.transpose(out=pv[:, k * 128:(k + 1) * 128], in_=self.xn[:, xi, k * 128:(k + 1) * 128],
                                               identity=self.idb[:]),
              reads=[("xn", xi), "idb"], writes=[("ps", bb)])
        ni = self.rr("nt", [0, 1])
        ntv = self.nt[:, ni, :].rearrange("p (k c) -> p k c", k=8)
        E("dve", lambda e: e.tensor_tensor(out=ntv, in0=pv.rearrange("p (k c) -> p k c", k=8),
                                           in1=self.gs[:, 2 * n, :].unsqueeze(2).to_broadcast([128, 8, 128]), op=ALU.mult),
          reads=[("ps", bb), ("gs", 2 * n)], writes=[("nt", ni)])
        E("dve", lambda e: e.tensor_tensor(out=self.hT[:, :, t * 128:(t + 1) * 128], in0=ntv,
                                            in1=self.gs[:, 2 * n + 1, :].unsqueeze(2).to_broadcast([128, 8, 128]), op=ALU.add),
          reads=[("nt", ni), ("gs", 2 * n + 1)], writes=[("hT", t)])

    def norm_tile(self, t, n, banks=(0, 1, 2, 3)):
        xi = self.norm_a(t)
        self.norm_b(t, n, xi, banks)

    def norm_pipe(self, n, delay=1, lazy_from=10 ** 9):
        pend = []

        def after_tile(t):
            while len(pend) >= delay and pend and pend[0][0] < lazy_from:
                t0, xi0 = pend.pop(0)
                self.norm_b(t0, n, xi0)
            pend.append((t, self.norm_a(t)))

        def ensure(tmax, banks=(4, 5, 6, 7)):
            while pend and pend[0][0] <= tmax:
                t0, xi0 = pend.pop(0)
                self.norm_b(t0, n, xi0, banks)

        def flush():
            ensure(10 ** 9, (0, 1, 2, 3))

        return after_tile, ensure, flush

    def ffn(self, g, fi, lo, extra=None, after_tile=None, next_loads=None, ensure=None):
        E = self.E
        ns = self.cfg.groups[g] + 1
        slots = list(range(lo, ns))
        n = 0 if fi == 0 else 2
        chunks = self.nchunks(lo, ns)
        actT = self.R
        for hh in range(2):
            loaded = {}

            def a5_load(jj, hh=hh):
                j = hh * NJH + jj
                slot, key = self.ring_load(self.d_wgu[fi][j], 8 * 256, ("wgu", fi, j))
                loaded[jj] = (slot[:, 0:2048].rearrange("p (k c) -> p k c", k=8), key)
                if jj == 2:
                    self.gid += 1
                    for part, (a, b_) in enumerate([(0, 4), (4, 8), (8, 11)]):
                        E("pool", lambda e, a=a, b_=b_, hh=hh: e.dma_start(
                            out=self.wdh[:, a:b_, :].rearrange("p a b -> p (a b)"),
                            in_=self.d_wdh[fi][hh][:, a * 1024:b_ * 1024]),
                          writes=[("wdh", part)], dma_sem="wdh", group=self.newgroup())

            def a5_step(jj, c0, c1, hh=hh):
                sv, key = loaded[jj]
                nn = c1 - c0
                if ensure is not None and hh == 0:
                    ensure(self.slots_of(c0, c1)[-1])
                ba, bb = self.rr("psA5", [(0, 1), (2, 3)])
                rd = [key] + [("hT", t) for t in self.slots_of(c0, c1)]
                for half, b in ((0, ba), (1, bb)):
                    for k in range(8):
                        E("pe", lambda e, half=half, b=b, k=k, sv=sv, c0=c0, c1=c1, nn=nn: e.matmul(
                            out=self.bank(b, nn), lhsT=sv[:, k, half * 128:(half + 1) * 128],
                            rhs=self.hT[:, k, c0:c1], start=(k == 0), stop=(k == 7)),
                          reads=rd, writes=[("ps", b)])
                tp, tk = self.tmp512()
                E("act", lambda e, tp=tp, ba=ba, nn=nn: e.activation(out=tp[:, 0:nn], in_=self.bank(ba, nn), func=AF.Silu),
                  reads=[("ps", ba)], writes=[tk])
                E("dve", lambda e, tp=tp, bb=bb, nn=nn, jj=jj, c0=c0, c1=c1: e.tensor_tensor(
                    out=actT[:, jj, c0:c1], in0=tp[:, 0:nn], in1=self.bank(bb, nn), op=ALU.mult),
                  reads=[tk, ("ps", bb)], writes=[("R", jj, t) for t in self.slots_of(c0, c1)])

            nskew = self.cfg.nring if (hh == 0 and ensure is not None and len(chunks) > 1) else 0
            if nskew:
                for jj in range(nskew):
                    a5_load(jj)
                for ci, (c0, c1) in enumerate(chunks):
                    for jj in range(nskew):
                        a5_step(jj, c0, c1)
            for jj in range(nskew, NJH):
                a5_load(jj)
                for (c0, c1) in chunks:
                    a5_step(jj, c0, c1)
            wkeys = [("wdh", p) for p in range(3)]
            if hh == 1 and after_tile is not None:
                self.zero_ss()
            if extra is None:
                if hh == 0:
                    self.prefetch([(self.d_wgu[fi][j_], 8 * 256, ("wgu", fi, j_)) for j_ in range(NJH, NJH + 3)])
                elif next_loads:
                    self.prefetch(next_loads)
            for t in slots:
                b0, b1 = self.rr("psA6", [(4, 5), (6, 7)])
                for jj in range(NJH):
                    for oh, b in ((0, b0), (1, b1)):
                        E("pe", lambda e, jj=jj, oh=oh, b=b, t=t: e.matmul(
                            out=self.bank(b), lhsT=actT[:, jj, t * 128:(t + 1) * 128],
                            rhs=self.wdh[:, jj, oh * 512:(oh + 1) * 512], start=(jj == 0), stop=(jj == NJH - 1)),
                          reads=wkeys + [("R", jj, t)], writes=[("ps", b)])
                for oh, b in ((0, b0), (1, b1)):
                    tp, tk = self.tmp512()
                    E("dve", lambda e, tp=tp, b=b, oh=oh: e.tensor_tensor(
                        out=tp, in0=self.bank(b), in1=self.bc[:, n, oh * 512:(oh + 1) * 512], op=ALU.mult),
                      reads=[("ps", b), ("bc", n)], writes=[tk])
                    E("dve", lambda e, tp=tp, oh=oh, t=t: e.tensor_tensor(
                        out=self.xres[:, t, oh * 512:(oh + 1) * 512], in0=self.xres[:, t, oh * 512:(oh + 1) * 512],
                        in1=tp, op=ALU.add),
                      reads=[tk, ("x", t)], writes=[("x", t)])
                if extra is not None:
                    extra()
                if hh == 1 and after_tile is not None:
                    after_tile(t)
            if hh == 0 and extra is not None:
                extra(upto=15)

    def rope_evac(self, bq, bs, dst, c0, c1):
        E = self.E
        nn = c1 - c0
        t1, k1 = self.tmp512()
        t2, k2 = self.tmp512()
        E("dve", lambda e: e.tensor_tensor(out=t1[:, 0:nn], in0=self.bank(bq, nn), in1=self.cosT[:, c0:c1], op=ALU.mult),
          reads=[("ps", bq), "tab"], writes=[k1])
        E("dve", lambda e: e.tensor_tensor(out=t2[:, 0:nn], in0=self.bank(bs, nn), in1=self.sinT[:, c0:c1], op=ALU.mult),
          reads=[("ps", bs), "tab"], writes=[k2])
        return t1, k1, t2, k2

    def mixer(self, g, after_tile=None, ensure=None):
        E = self.E
        cfg = self.cfg
        nt = cfg.groups[g]
        ns = nt + 1
        lo = 0 if g == 0 else 1
        tile0 = sum(cfg.groups[:g])
        Y = self.R
        MB = 8
        for dst, src in ((self.cosT, self.d_cos), (self.sinT, self.d_sin)):
            E("sp", lambda e, dst=dst, src=src: e.dma_start(out=dst[:, 0:ns * 128], in_=src[:, tile0 * 128:(tile0 + ns) * 128]),
              writes=["tab"], dma_sem="tab", group=("tab", g))
        if g > 0:
            pns = cfg.groups[g - 1] + 1
            E("dve", lambda e: e.tensor_copy(out=self.kT[:, :, 0:128], in_=self.kT[:, :, (pns - 1) * 128:pns * 128]),
              reads=[("kT", gg, pns - 1) for gg in range(4)], writes=[("kT", gg, 0) for gg in range(4)])
            E("dve", lambda e: e.tensor_copy(out=self.vv[:, 0, :], in_=self.vv[:, pns - 1, :]),
              reads=[("v", pns - 1)], writes=[("v", 0)])
        chunks_lo = self.nchunks(lo, ns)
        chunks_1 = self.nchunks(1, ns)
        for j in range(8):
            slot, key = self.ring_load(self.d_wcv[j], 8 * 384, ("wcv", j))
            if j == 2:
                E("pool", lambda e: e.dma_start(out=self.wdh[:, 0:8, :].rearrange("p a b -> p (a b)"), in_=self.d_wout),
                  writes=[("wdh", p) for p in range(3)], dma_sem="wdh", group=self.newgroup(True))
            sv = slot[:, 0:3072].rearrange("p (k c) -> p k c", k=8)
            ci = 0
            cu = self.cu[:, ci, :]
            if g > 0:
                E("dve", lambda e, cu=cu, j=j: e.tensor_copy(out=cu[:, 128:130], in_=self.cucarry[:, j, :]),
                  reads=[("cucarry", j)], writes=[("cu", ci)])
            else:
                E("dve", lambda e, cu=cu: e.memset(cu[:, 0:2], 0.0), writes=[("cu", ci)])
            for ich, (c0, c1) in enumerate(chunks_lo):
                nn = c1 - c0
                if ensure is not None and j == 0:
                    ensure(self.slots_of(c0, c1)[-1], (6, 7))
                b_b, b_c, b_u = self.rr("psCV", [(0, 1, 2), (3, 4, 5)])
                rd = [key] + [("hT", t) for t in self.slots_of(c0, c1)]
                for m, b in ((0, b_b), (1, b_c), (2, b_u)):
                    for k in range(8):
                        E("pe", lambda e, m=m, b=b, k=k, sv=sv, c0=c0, c1=c1, nn=nn: e.matmul(
                            out=self.bank(b, nn), lhsT=sv[:, k, m * 128:(m + 1) * 128], rhs=self.hT[:, k, c0:c1],
                            start=(k == 0), stop=(k == 7)),
                          reads=rd, writes=[("ps", b)])
                ut, uk = self.tmp512()
                E("act", lambda e, ut=ut, b_u=b_u, nn=nn: e.activation(out=ut[:, 0:nn], in_=self.bank(b_u, nn), func=AF.Copy),
                  reads=[("ps", b_u)], writes=[uk])
                bt_, bk_ = self.tmp512()
                E("act", lambda e, bt_=bt_, b_b=b_b, nn=nn: e.activation(out=bt_[:, 0:nn], in_=self.bank(b_b, nn), func=AF.Copy),
                  reads=[("ps", b_b)], writes=[bk_])
                E("dve", lambda e, cu=cu, ut=ut, b_c=b_c, c0=c0, c1=c1, nn=nn: e.tensor_tensor(
                    out=cu[:, 2 + c0:2 + c1], in0=self.bank(b_c, nn), in1=ut[:, 0:nn], op=ALU.mult),
                  reads=[("ps", b_c), uk, ("cu", ci)], writes=[("cu", ci)])
                if g == 0 and ich == 0:
                    E("dve", lambda e, cu=cu: e.tensor_scalar(out=cu[:, 2 + 126:2 + 128], in0=cu[:, 2 + 126:2 + 128],
                                                              scalar1=self.flag[:, 0:1], scalar2=None, op0=ALU.mult),
                      reads=[("cu", ci), "flag"], writes=[("cu", ci)])
                at, ak = self.tmp512()
                E("dve", lambda e, at=at, cu=cu, j=j, c0=c0, c1=c1, nn=nn: e.tensor_scalar(
                    out=at[:, 0:nn], in0=cu[:, 2 + c0:2 + c1], scalar1=self.convw[:, j * 3 + 2:j * 3 + 3], scalar2=None,
                    op0=ALU.mult), reads=[("cu", ci), "convw"], writes=[ak])
                for kk, off in ((1, 1), (0, 0)):
                    E("dve", lambda e, at=at, cu=cu, j=j, c0=c0, c1=c1, nn=nn, kk=kk, off=off: e.scalar_tensor_tensor(
                        out=at[:, 0:nn], in0=cu[:, off + c0:off + c1], scalar=self.convw[:, j * 3 + kk:j * 3 + kk + 1],
                        in1=at[:, 0:nn], op0=ALU.mult, op1=ALU.add), reads=[("cu", ci), ak, "convw"], writes=[ak])
                E("dve", lambda e, at=at, bt_=bt_, j=j, c0=c0, c1=c1, nn=nn: e.tensor_tensor(
                    out=Y[:, j, c0:c1], in0=bt_[:, 0:nn], in1=at[:, 0:nn], op=ALU.mult),
                  reads=[bk_, ak], writes=[("R", j, t) for t in self.slots_of(c0, c1)])
            E("dve", lambda e, cu=cu, j=j: e.tensor_copy(out=self.cucarry[:, j, :], in_=cu[:, ns * 128:ns * 128 + 2]),
              reads=[("cu", ci)], writes=[("cucarry", j)])
        self.gated_proj(self.d_wcp, "wcp", chunks_1, first=True)
        for c in range(8):
            slot, key = self.ring_load(self.d_wq[c], 8 * 256)
            sv = slot[:, 0:2048].rearrange("p (k c) -> p k c", k=8)
            for (c0, c1) in chunks_1:
                nn = c1 - c0
                bq, bs = self.rr("psQ", [(0, 1), (2, 3)])
                rd = [key] + [("hT", t) for t in self.slots_of(c0, c1)]
                for m, b in ((0, bq), (1, bs)):
                    for k in range(8):
                        E("pe", lambda e, m=m, b=b, k=k, sv=sv, c0=c0, c1=c1, nn=nn: e.matmul(
                            out=self.bank(b, nn), lhsT=sv[:, k, m * 128:(m + 1) * 128], rhs=self.hT[:, k, c0:c1],
                            start=(k == 0), stop=(k == 7)), reads=rd, writes=[("ps", b)])
                t1, k1, t2, k2 = self.rope_evac(bq, bs, None, c0, c1)
                E("dve", lambda e, t1=t1, t2=t2, c=c, c0=c0, c1=c1, nn=nn: e.tensor_tensor(
                    out=Y[:, c, c0:c1], in0=t1[:, 0:nn], in1=t2[:, 0:nn], op=ALU.add),
                  reads=[k1, k2], writes=[("R", c, t) for t in self.slots_of(c0, c1)])
        for kg in range(4):
            slot, key = self.ring_load(self.d_wk[kg], 8 * 256)
            sv = slot[:, 0:2048].rearrange("p (k c) -> p k c", k=8)
            for (c0, c1) in chunks_lo:
                nn = c1 - c0
                bq, bs = self.rr("psQ", [(0, 1), (2, 3)])
                rd = [key] + [("hT", t) for t in self.slots_of(c0, c1)]
                for m, b in ((0, bq), (1, bs)):
                    for k in range(8):
                        E("pe", lambda e, m=m, b=b, k=k, sv=sv, c0=c0, c1=c1, nn=nn: e.matmul(
                            out=self.bank(b, nn), lhsT=sv[:, k, m * 128:(m + 1) * 128], rhs=self.hT[:, k, c0:c1],
                            start=(k == 0), stop=(k == 7)), reads=rd, writes=[("ps", b)])
                t1, k1, t2, k2 = self.rope_evac(bq, bs, None, c0, c1)
                E("dve", lambda e, t1=t1, t2=t2, kg=kg, c0=c0, c1=c1, nn=nn: e.tensor_tensor(
                    out=self.kT[:, kg, c0:c1], in0=t1[:, 0:nn], in1=t2[:, 0:nn], op=ALU.add),
                  reads=[k1, k2], writes=[("kT", kg, t) for t in self.slots_of(c0, c1)])
        slot, key = self.ring_load(self.d_wv[0], 8 * 256)
        sv = slot[:, 0:2048].rearrange("p (k c) -> p k c", k=8)
        for t in range(lo, ns):
            b = self.rr("psV", [0, 1, 2, 3])
            for k in range(8):
                E("pe", lambda e, k=k, b=b, t=t, sv=sv: e.matmul(
                    out=self.bank(b, 256), lhsT=self.hT[:, k, t * 128:(t + 1) * 128], rhs=sv[:, k, :],
                    start=(k == 0), stop=(k == 7)), reads=[key, ("hT", t)], writes=[("ps", b)])
            E("act", lambda e, b=b, t=t: e.activation(out=self.vv[:, t, :], in_=self.bank(b, 256), func=AF.Copy),
              reads=[("ps", b)], writes=[("v", t)])
        if cfg.stop == 3 and getattr(cfg, "sub", 9) < 1:
            return
        self.prefetch([(self.d_wap[j_], 8 * 256, ("wap", j_)) for j_ in range(3)])
        self.attention(g, ns)
        if cfg.stop == 3 and getattr(cfg, "sub", 9) < 2:
            return
        self.gated_proj(self.d_wap, "wap", chunks_1, first=False)
        wkeys = [("wdh", p) for p in range(3)]
        self.prefetch([(self.d_wgu[1][j_], 8 * 256, ("wgu", 1, j_)) for j_ in range(3)])
        if after_tile is not None:
            self.zero_ss()
        for t in range(1, ns):
            b0, b1 = self.rr("psA6", [(4, 5), (6, 7)])
            for k in range(8):
                for oh, b in ((0, b0), (1, b1)):
                    E("pe", lambda e, k=k, oh=oh, b=b, t=t: e.matmul(
                        out=self.bank(b), lhsT=self.R[:, MB + k, t * 128:(t + 1) * 128],
                        rhs=self.wdh[:, k, oh * 512:(oh + 1) * 512], start=(k == 0), stop=(k == 7)),
                      reads=wkeys + [("R", MB + k, t)], writes=[("ps", b)])
            for oh, b in ((0, b0), (1, b1)):
                tp, tk = self.tmp512()
                E("dve", lambda e, tp=tp, b=b, oh=oh: e.tensor_tensor(
                    out=tp, in0=self.bank(b), in1=self.bc[:, 1, oh * 512:(oh + 1) * 512], op=ALU.mult),
                  reads=[("ps", b), ("bc", 1)], writes=[tk])
                E("dve", lambda e, tp=tp, oh=oh, t=t: e.tensor_tensor(
                    out=self.xres[:, t, oh * 512:(oh + 1) * 512], in0=self.xres[:, t, oh * 512:(oh + 1) * 512],
                    in1=tp, op=ALU.add), reads=[tk, ("x", t)], writes=[("x", t)])
            if after_tile is not None:
                after_tile(t)

    def gated_proj(self, dsrc, dtag, chunks, first):
        E = self.E
        Y = self.R
        MB = 8
        for j in range(8):
            slot, key = self.ring_load(dsrc[j], 8 * 256, (dtag, j))
            sv = slot[:, 0:2048].rearrange("p (k c) -> p k c", k=8)
            for (c0, c1) in chunks:
                nn = c1 - c0
                by, bz = self.rr("psQ", [(0, 1), (2, 3)])
                sl = self.slots_of(c0, c1)
                for k in range(8):
                    E("pe", lambda e, k=k, by=by, sv=sv, c0=c0, c1=c1, nn=nn: e.matmul(
                        out=self.bank(by, nn), lhsT=sv[:, k, 0:128], rhs=Y[:, k, c0:c1], start=(k == 0), stop=(k == 7)),
                      reads=[key] + [("R", k, t) for t in sl], writes=[("ps", by)])
                for k in range(8):
                    E("pe", lambda e, k=k, bz=bz, sv=sv, c0=c0, c1=c1, nn=nn: e.matmul(
                        out=self.bank(bz, nn), lhsT=sv[:, k, 128:256], rhs=self.hT[:, k, c0:c1], start=(k == 0), stop=(k == 7)),
                      reads=[key] + [("hT", t) for t in sl], writes=[("ps", bz)])
                sg, sk = self.tmp512()
                E("act", lambda e, sg=sg, bz=bz, nn=nn: e.activation(out=sg[:, 0:nn], in_=self.bank(bz, nn), func=AF.Sigmoid),
                  reads=[("ps", bz)], writes=[sk])
                mk = [("R", MB + j, t) for t in sl]
                if first:
                    E("dve", lambda e, sg=sg, by=by, j=j, c0=c0, c1=c1, nn=nn: e.tensor_tensor(
                        out=self.R[:, MB + j, c0:c1], in0=self.bank(by, nn), in1=sg[:, 0:nn], op=ALU.mult),
                      reads=[("ps", by), sk], writes=mk)
                else:
                    E("dve", lambda e, sg=sg, by=by, nn=nn: e.tensor_tensor(
                        out=sg[:, 0:nn], in0=self.bank(by, nn), in1=sg[:, 0:nn], op=ALU.mult),
                      reads=[("ps", by), sk], writes=[sk])
                    E("dve", lambda e, sg=sg, j=j, c0=c0, c1=c1, nn=nn: e.tensor_tensor(
                        out=self.R[:, MB + j, c0:c1], in0=self.R[:, MB + j, c0:c1], in1=sg[:, 0:nn], op=ALU.add),
                      reads=[sk] + mk, writes=mk)

    def attention(self, g, ns):
        E = self.E
        Y = self.R
        yb = (6, 7)
        units = [(t, kg) for t in range(1, ns) for kg in range(4)]
        st = {}

        def tile_state(t):
            if t not in st:
                par = self.rr("astp", [0, 1])
                st[t] = dict(par=par, negm=self.ast[:, par, 0, :], rsum=self.ast[:, par, 1, :],
                             atmp=self.ast[:, par, 2, :], rden=self.ast[:, par, 3, :], akey=("ast", par))
                E("dve", lambda e, r=st[t]["rsum"]: e.memset(r, 0.0), writes=[("rsum", par, k_) for k_ in range(4)])
            return st[t]

        ust = {}

        def S1(u):
            t, kg = units[u]
            tile_state(t)
            b0, b1 = self.rr("psS", [(0, 1), (2, 3)])
            si = self.rr("Sm", [0, 1])
            ust[u] = dict(b0=b0, b1=b1, si=si)
            tc0, tc1 = t * 128, (t + 1) * 128
            for hh in range(4):
                ch, ph = 2 * kg + (hh % 2), (hh // 2) * 64
                b = b0 if hh < 2 else b1
                for kb in range(2):
                    o0 = (hh % 2) * 256 + kb * 128
                    ks = t - 1 + kb
                    E("pe", lambda e, b=b, o0=o0, ch=ch, ph=ph, ks=ks, kg=kg, tc0=tc0, tc1=tc1: e.matmul(
                        out=self.ps[:, b * 512 + o0:b * 512 + o0 + 128], lhsT=Y[ph:ph + 64, ch, tc0:tc1],
                        rhs=self.kT[ph:ph + 64, kg, ks * 128:(ks + 1) * 128], start=True, stop=True),
                      reads=[("R", ch, t), ("kT", kg, ks)], writes=[("ps", b)])

        def S2(u):
            t, kg = units[u]
            ts_ = st[t]
            negm, rsum, par = ts_["negm"], ts_["rsum"], ts_["par"]
            nk, rk = ("negm", par, kg), ("rsum", par, kg)
            b0, b1, si = ust[u]["b0"], ust[u]["b1"], ust[u]["si"]
            mi = 0 if (g == 0 and t == 1) else 1
            for half, b in ((0, b0), (1, b1)):
                E("dve", lambda e, half=half, b=b, si=si, mi=mi: e.scalar_tensor_tensor(
                    out=self.Sm[:, si, 2 * half:2 * half + 2, :],
                    in0=self.bank(b).rearrange("p (a b) -> p a b", a=2), scalar=HD ** -0.5,
                    in1=self.masks[:, mi:mi + 1, :].to_broadcast([128, 2, 256]), op0=ALU.mult, op1=ALU.add),
                  reads=[("ps", b), "masks"], writes=[("Sm", si, half)])
            E("dve", lambda e, si=si, kg=kg, negm=negm: e.tensor_reduce(out=negm[:, 4 * kg:4 * kg + 4], in_=self.Sm[:, si, :, :],
                                                                        axis=AX.X, op=ALU.max),
              reads=[("Sm", si, 0), ("Sm", si, 1)], writes=[nk])
            E("dve", lambda e, kg=kg, negm=negm: e.scalar_tensor_tensor(
                out=negm[:, 4 * kg:4 * kg + 4], in0=negm[:, 4 * kg:4 * kg + 4], scalar=-1.0,
                in1=self.sinkbc[:, 1, 4 * kg:4 * kg + 4], op0=ALU.mult, op1=ALU.min),
              reads=[nk, "sink1"], writes=[nk])
            for hh in range(4):
                h = 4 * kg + hh
                E("act", lambda e, si=si, hh=hh, h=h, negm=negm, rsum=rsum: e.activation(
                    out=self.P[:, si, hh, :], in_=self.Sm[:, si, hh, :], func=AF.Exp, bias=negm[:, h:h + 1],
                    accum_out=rsum[:, h:h + 1]),
                  reads=[("Sm", si, hh // 2), nk, rk], writes=[("P", si, hh), rk])

        def S3a(u):
            si = ust[u]["si"]
            bt = self.rr("psPT", [4, 5])
            ptv = self.bank_bf(bt)
            for hh in range(4):
                for kb in range(2):
                    o = (hh * 2 + kb) * 128
                    E("pe", lambda e, si=si, hh=hh, kb=kb, o=o, ptv=ptv: e.transpose(
                        out=ptv[:, o:o + 128], in_=self.P[:, si, hh, kb * 128:(kb + 1) * 128], identity=self.idb[:]),
                      reads=[("P", si, hh), "idb"], writes=[("ps", bt)])
            if u % 2 == 0:
                E("dve", lambda e, si=si, ptv=ptv: e.tensor_copy(out=self.pT[:, si, :], in_=ptv),
                  reads=[("ps", bt)], writes=[("pT", si)])
            else:
                E("act", lambda e, si=si, ptv=ptv: e.activation(out=self.pT[:, si, :], in_=ptv, func=AF.Copy),
                  reads=[("ps", bt)], writes=[("pT", si)])

        def S3b(u):
            t, kg = units[u]
            si = ust[u]["si"]
            for hh in range(4):
                h = 4 * kg + hh
                b = yb[h // 8]
                for kb in range(2):
                    o = (hh * 2 + kb) * 128
                    ks = t - 1 + kb
                    E("pe", lambda e, si=si, o=o, ks=ks, kg=kg, h=h, kb=kb, b=b: e.matmul(
                        out=self.ps[:, b * 512 + (h % 8) * 64:b * 512 + (h % 8 + 1) * 64], lhsT=self.pT[:, si, o:o + 128],
                        rhs=self.vv[:, ks, kg * 64:(kg + 1) * 64], start=(kb == 0), stop=(kb == 1)),
                      reads=[("pT", si), ("v", ks)], writes=[("ps", b)])
            if kg == 3:
                tile_end_a1(t)

        def tile_end_a1(t):
            ts_ = st[t]
            negm, atmp, par = ts_["negm"], ts_["atmp"], ts_["par"]
            nks = [("negm", par, k_) for k_ in range(4)]
            tk_ = ("atmp", par)
            E("dve", lambda e: e.tensor_tensor(out=atmp, in0=negm, in1=self.sinkbc[:, 0, :], op=ALU.add),
              reads=nks + ["sink0"], writes=[tk_])
            E("act", lambda e: e.activation(out=atmp, in_=atmp, func=AF.Exp), reads=[tk_], writes=[tk_])

        def tile_end_a2(t):
            ts_ = st[t]
            negm, rsum, atmp, rden, par = ts_["negm"], ts_["rsum"], ts_["atmp"], ts_["rden"], ts_["par"]
            nks = [("negm", par, k_) for k_ in range(4)]
            rks = [("rsum", par, k_) for k_ in range(4)]
            tk_, dk_ = ("atmp", par), ("rden", par)
            E("dve", lambda e: e.tensor_tensor(out=atmp, in0=atmp, in1=rsum, op=ALU.add), reads=[tk_] + rks, writes=[tk_])
            E("dve", lambda e: e.reciprocal(out=rden, in_=atmp), reads=[tk_], writes=[dk_])
            for half in range(2):
                E("dve", lambda e, half=half: e.tensor_tensor(
                    out=self.ybf[:, half * 512:(half + 1) * 512].rearrange("p (a b) -> p a b", a=8),
                    in0=self.bank(yb[half]).rearrange("p (a b) -> p a b", a=8),
                    in1=rden[:, 8 * half:8 * half + 8].unsqueeze(2).to_broadcast([128, 8, 64]), op=ALU.mult),
                  reads=[("ps", yb[half]), dk_], writes=[("ybf", half)])

        def tile_end_b(t):
            tc0, tc1 = t * 128, (t + 1) * 128
            bt = self.rr("psPT", [4, 5])
            ytv = self.bank_bf(bt)
            for k in range(8):
                E("pe", lambda e, k=k, ytv=ytv: e.transpose(out=ytv[:, k * 128:(k + 1) * 128],
                                                            in_=self.ybf[:, k * 128:(k + 1) * 128], identity=self.idb[:]),
                  reads=[("ybf", k // 4), "idb"], writes=[("ps", bt)])
            E("act", lambda e, ytv=ytv: e.activation(out=Y[:, 0:8, tc0:tc1], in_=ytv.rearrange("p (a b) -> p a b", a=8),
                                                     func=AF.Copy),
              reads=[("ps", bt)], writes=[("R", k, t) for k in range(8)])

        n = len(units)
        for s_ in range(n + 6):
            if s_ < n:
                S1(s_)
            if 0 <= s_ - 1 < n:
                S2(s_ - 1)
            if 0 <= s_ - 2 < n:
                S3a(s_ - 2)
            if 0 <= s_ - 4 < n and units[s_ - 4][1] == 3:
                tile_end_a2(units[s_ - 4][0])
            if 0 <= s_ - 3 < n:
                S3b(s_ - 3)
            if 0 <= s_ - 5 < n and units[s_ - 5][1] == 3:
                tile_end_b(units[s_ - 5][0])

    def final_stats(self, t):
        oi = self.rr("nt", [0, 1])
        fc = self.cfg.nsmax + t
        E = self.E
        E("act", lambda e: e.activation(out=self.nt[:, oi, :].bitcast(BF16)[:, 0:D], in_=self.xres[:, t, :], func=AF.Square,
                                        accum_out=self.stat[:, 0, fc:fc + 1]),
          reads=[("x", t), ("ss", fc)], writes=[("ss", fc), ("nt", oi)])
        E("act", lambda e: e.activation(out=self.stat[:, 1, fc:fc + 1], in_=self.stat[:, 0, fc:fc + 1], func=AF.Sqrt,
                                        bias=self.epsc[:, 0:1], scale=1.0 / D),
          reads=[("ss", fc), "epsc"], writes=[("rstd", fc)])
        return oi

    def final_finish(self, g, t, oi):
        E = self.E
        tile0 = sum(self.cfg.groups[:g])
        fc = self.cfg.nsmax + t
        E("dve", lambda e: e.reciprocal(out=self.stat[:, 1, fc:fc + 1], in_=self.stat[:, 1, fc:fc + 1]),
          reads=[("rstd", fc)], writes=[("rstd", fc)])
        E("dve", lambda e: e.scalar_tensor_tensor(
            out=self.nt[:, oi, :], in0=self.xres[:, t, :], scalar=self.stat[:, 1, fc:fc + 1], in1=self.bc[:, 3, :],
            op0=ALU.mult, op1=ALU.mult), reads=[("x", t), ("rstd", fc), ("bc", 3)], writes=[("nt", oi)])
        E("sp", lambda e: e.dma_start(out=self.d_out[tile0 + t - 1], in_=self.nt[:, oi, :]),
          reads=[("nt", oi)], writes=[("out", tile0 + t - 1)], dma_sem="st%d" % oi)

    def load_x_tile(self, g, t):
        tile0 = sum(self.cfg.groups[:g])
        self.E("sp", lambda e: e.dma_start(out=self.xres[:, t, :], in_=self.d_x[tile0 + t]),
               writes=[("x", t)], dma_sem="x%d" % t)

    def build(self):
        nc, cfg = self.nc, self.cfg
        self.declare_dram()
        with ExitStack() as es:
            es.enter_context(nc.allow_low_precision("bf16 matmul operands, fp32 accumulation"))
            self.alloc(es)
            self.setup()
            ng = len(cfg.groups)
            for t in range(0, cfg.groups[0] + 1):
                self.load_x_tile(0, t)
            self.zero_ss()
            nearly = min(4, cfg.groups[0] + 1)
            early = [(t, self.norm_a(t)) for t in range(nearly)]
            for l in range(8):
                self.ada_load(l)
            self.ada_finish(0)
            for t, xi in early:
                self.norm_b(t, 0, xi, banks=(4, 5, 6))
            for t in range(nearly, cfg.groups[0] + 1):
                self.norm_tile(t, 0, banks=(4, 5, 6))
            pending = list(range(8, N_ADA_LOADS))

            def extra(upto=None):
                while pending:
                    l = pending.pop(0)
                    self.ada_load(l)
                    if l == 15:
                        self.ada_finish(1)
                    if l == 23:
                        self.ada_finish(2)
                    if upto is None or l >= upto:
                        break

            ens_prev = None
            for g in range(ng):
                lo = 0 if g == 0 else 1
                ns = cfg.groups[g] + 1
                at, ens1, fl = self.norm_pipe(1, lazy_from=(5 if g > 0 else 4))
                self.ffn(g, 0, lo, extra=extra if g == 0 else None, after_tile=at, ensure=ens_prev,
                         next_loads=[(self.d_wcv[j_], 8 * 384, ("wcv", j_)) for j_ in range(3)])
                while pending:
                    extra()
                at, ens2, fl2 = self.norm_pipe(2, lazy_from=5)
                self.mixer(g, after_tile=at, ensure=ens1)
                fl()
                nns = cfg.groups[g + 1] + 1 if g + 1 < ng else 0
                at0, ens0, fl0 = self.norm_pipe(0, delay=1, lazy_from=5)

                xl_pend = []

                def after_ffn2(t, g=g, nns=nns, at0=at0, xl_pend=xl_pend):
                    oi = self.final_stats(t)
                    if xl_pend:
                        at0(xl_pend.pop(0))
                    self.final_finish(g, t, oi)
                    if t < nns:
                        self.load_x_tile(g + 1, t)
                        xl_pend.append(t)

                self.ffn(g, 1, 1, after_tile=after_ffn2, ensure=ens2,
                         next_loads=[(self.d_wgu[0][j_], 8 * 256, ("wgu", 0, j_)) for j_ in range(3)] if g + 1 < ng else None)
                fl2()
                while xl_pend:
                    at0(xl_pend.pop(0))
                for t in range(ns, nns):
                    self.load_x_tile(g + 1, t)
                    at0(t)
                ens_prev = ens0
            self.E("sp", None, reads=[("out", i) for i in range(cfg.ntc)])
            streams = self.tr.finalize()
            sems = {}
            for e_ in Tracker.ENGS:
                sems[("E", e_)] = es.enter_context(nc.semaphore("s_" + e_))
            for k in self.tr.dma_sems:
                sems[("D", k)] = es.enter_context(nc.semaphore("d_" + k))
            block = es.enter_context(nc.Block())

            def replay(eng, name):
                mysem = sems[("E", name)]
                for item in streams[name]:
                    if item[0] == "wait":
                        eng.wait_ge(sems[item[1]], item[2])
                    else:
                        op = item[1]
                        ins = op.fn(eng)
                        if op.dma_sem is not None:
                            ins.then_inc(sems[("D", op.dma_sem)], 16)
                        elif op.needed:
                            ins.then_inc(mysem, 1)

            @block.tensor
            def _(e):
                replay(e, "pe")

            @block.vector
            def _(e):
                replay(e, "dve")

            @block.scalar
            def _(e):
                replay(e, "act")

            @block.gpsimd
            def _(e):
                replay(e, "pool")

            @block.sync
            def _(e):
                replay(e, "sp")
        return nc


def _pack(W, cols):
    K = W.shape[0]
    Wc = W[:, cols]
    return np.ascontiguousarray(Wc.reshape(K // 128, 128, len(cols)).transpose(1, 0, 2)).reshape(128, -1)


def _col(v):
    return np.ascontiguousarray(v.reshape(-1, 128).T)


def prepare_weights(inp):
    f = lambda a: np.asarray(a, dtype=np.float32)
    ar = np.arange
    w_ada = f(inp["w_ada"])[0]
    w_in = f(inp["w_in"])[0]
    out = {}
    out["wada"] = np.stack([_pack(w_ada, ar(l * 384, (l + 1) * 384)) for l in range(N_ADA_LOADS)])
    for fi, (gu, dn) in enumerate((("w1_gu", "w1_down"), ("w2_gu", "w2_down"))):
        wgu = f(inp[gu])[0]
        wd = f(inp[dn])[0]
        out["wgu%d" % fi] = np.stack([_pack(wgu, np.concatenate([ar(j * 128, (j + 1) * 128), DFF + ar(j * 128, (j + 1) * 128)]))
                                      for j in range(NJ)])
        out["wdh%d" % fi] = np.stack([_pack(wd[hh * NJH * 128:(hh + 1) * NJH * 128], ar(D)) for hh in range(2)])
    out["wcv"] = np.stack([_pack(w_in, np.concatenate([O_B + ar(j * 128, (j + 1) * 128), O_C + ar(j * 128, (j + 1) * 128),
                                                        O_U + ar(j * 128, (j + 1) * 128)])) for j in range(8)])

    def swp(base):
        return np.concatenate([base + ar(32, 64), base + ar(0, 32)])

    def qheads(c):
        g_, e_ = c // 2, c % 2
        return 4 * g_ + e_, 4 * g_ + 2 + e_

    out["wq"] = np.stack([_pack(w_in, np.concatenate([O_Q + qheads(c)[0] * 64 + ar(64), O_Q + qheads(c)[1] * 64 + ar(64),
                                                       swp(O_Q + qheads(c)[0] * 64), swp(O_Q + qheads(c)[1] * 64)]))
                          for c in range(8)])
    out["wk"] = np.stack([_pack(w_in, np.concatenate([O_K + g * 64 + ar(64), O_K + g * 64 + ar(64),
                                                       swp(O_K + g * 64), swp(O_K + g * 64)])) for g in range(4)])
    out["wv"] = _pack(w_in, O_V + ar(256))[None]
    wcp = f(inp["w_conv_proj"])[0]
    wap = f(inp["w_attn_proj"])[0]
    out["wcp"] = np.stack([np.concatenate([_pack(wcp, ar(j * 128, (j + 1) * 128)).reshape(128, 8, 128),
                                            _pack(w_in, O_ZC + ar(j * 128, (j + 1) * 128)).reshape(128, 8, 128)],
                                           axis=2).reshape(128, -1) for j in range(8)])
    out["wap"] = np.stack([np.concatenate([_pack(wap, ar(j * 128, (j + 1) * 128)).reshape(128, 8, 128),
                                            _pack(w_in, O_ZA + ar(j * 128, (j + 1) * 128)).reshape(128, 8, 128)],
                                           axis=2).reshape(128, -1) for j in range(8)])
    out["wout"] = _pack(f(inp["w_out"])[0], ar(D))
    out["gcol"] = np.concatenate([_col(f(inp[n])[0]) for n in ("g_ffn1", "g_mix", "g_ffn2")], axis=1)
    out["badacol"] = _col(f(inp["b_ada"])[0])
    cw = f(inp["conv_w"])[0]
    out["convw"] = np.ascontiguousarray(cw.reshape(3, 8, 128).transpose(2, 1, 0)).reshape(128, 24)
    out["gfin"] = f(inp["g_final"])
    out["sinks"] = f(inp["sinks"])[0]
    out["ident"] = np.eye(128, dtype=np.float32)
    return out


def rope_tables_T(pos):
    inv = (1.0 / (ROPE_THETA ** (np.arange(0, HD, 2, dtype=np.float32) / np.float32(HD)))).astype(np.float32)
    ang = (pos.astype(np.float32)[None, :] * inv[:, None]).astype(np.float32)
    cos = np.cos(ang).astype(np.float32)
    sin = np.sin(ang).astype(np.float32)
    cosT = np.tile(cos, (4, 1))
    sgn = np.where((np.arange(128) % 64) < 32, -1.0, 1.0).astype(np.float32)
    sinT = np.tile(sin, (4, 1)) * sgn[:, None]
    return np.ascontiguousarray(cosT), np.ascontiguousarray(sinT)


def band_masks(first_half):
    qi = np.arange(128)[:, None]
    kj = np.arange(256)[None, :]
    diff = 128 + qi - kj
    valid = (diff >= 0) & (diff < 128)
    reg = np.where(valid, 0.0, NEG).astype(np.float32)
    m0 = reg.copy()
    if first_half:
        m0[:, 0:128] = NEG
    return np.ascontiguousarray(np.concatenate([m0, reg], axis=1))


def core_inputs(cfg, wts, x, c, b, s):
    n = cfg.ntc * 128
    xin = np.zeros((cfg.ntc + 1, 128, D), dtype=np.float32)
    xin[1:] = x[b, s * n:(s + 1) * n].reshape(cfg.ntc, 128, D)
    if s > 0:
        xin[0] = x[b, s * n - 128:s * n]
    pos = np.arange(s * n - 128, (s + 1) * n)
    cosT, sinT = rope_tables_T(pos)
    m = dict(wts)
    m["xin"] = xin
    m["ccol"] = _col(c[b])
    m["masks"] = band_masks(s == 0)
    m["flag"] = np.full((128, 1), 0.0 if s == 0 else 1.0, dtype=np.float32)
    m["cost"] = cosT
    m["sint"] = sinT
    return m


_NC_CACHE = {}


def run(cfg, inputs, nsplit, trace=False):
    x = np.asarray(inputs["x"], dtype=np.float32)
    c = np.asarray(inputs["c"], dtype=np.float32)
    B, S, _ = x.shape
    assert S == nsplit * cfg.ntc * 128
    wts = prepare_weights(inputs)
    in_maps = []
    for b in range(B):
        for s in range(nsplit):
            in_maps.append(core_inputs(cfg, wts, x, c, b, s))
    key = (cfg.ntc, tuple(cfg.groups), cfg.nring)
    if key not in _NC_CACHE:
        _NC_CACHE[key] = Builder(cfg).build()
    nc = _NC_CACHE[key]
    res = run_bass_kernel_spmd(nc, in_maps, core_ids=list(range(len(in_maps))), trace=trace)
    out = np.empty((B, S, D), dtype=np.float32)
    n = cfg.ntc * 128
    i = 0
    for b in range(B):
        for s in range(nsplit):
            out[b, s * n:(s + 1) * n] = np.asarray(res.results[i]["out"]).reshape(n, D)
            i += 1
    return out, res


def kernel(**inputs):
    cfg = Cfg(ntc=32, groups=(7, 7, 6, 6, 6), nring=3, ntmp=4)
    out, _ = run(cfg, inputs, nsplit=2)
    return out
```
